# Optimizing a Trainium2 kernel written in Bass

```python
import math
import jax, jax.numpy as jnp
from jax import lax
import numpy as np

D_MODEL = 1024
BATCH = 4
SEQ = 4096
DEPTH = 4

N_MIXERS = 3
BLOCK = 128
ROPE_THETA = 10000.0
NORM_EPS = 1e-6

A_HEADS = 8
A_HEAD_DIM = D_MODEL // (2 * A_HEADS)
A_V_DIM = 2 * A_HEAD_DIM
A_IN = 4 * A_HEADS * A_HEAD_DIM + A_HEADS * A_V_DIM

B_HEADS = 16
B_HEAD_DIM = D_MODEL // B_HEADS
B_IN = 3 * B_HEADS * B_HEAD_DIM

C_HEADS = 16
C_KV_HEADS = 2
C_GROUP = C_HEADS // C_KV_HEADS
C_HEAD_DIM = 64
WINDOW = 128
C_IN = (C_HEADS + 2 * C_KV_HEADS) * C_HEAD_DIM

ROPE_DIM = 64

FFN_HIDDEN = -(-8 * D_MODEL // (3 * 256)) * 256

N_A = (DEPTH + 2) // 3
N_B = (DEPTH + 1) // 3
N_C = DEPTH // 3

kernel_name = "hybrid_diff_stickbreak_swasink_trunk"


def rms_norm(x, g):
    xf = x.astype(jnp.float32)
    y = xf * lax.rsqrt(jnp.mean(xf * xf, axis=-1, keepdims=True) + NORM_EPS)
    return (y * g.astype(jnp.float32)).astype(x.dtype)


def rope_tables(positions):
    inv = ROPE_THETA ** (-jnp.arange(0, ROPE_DIM, 2, dtype=jnp.float32) / ROPE_DIM)
    ang = positions.astype(jnp.float32)[..., None] * inv
    return jnp.cos(ang), jnp.sin(ang)


def apply_rope(x, cos, sin):
    xf = x.astype(jnp.float32)
    x1, x2 = jnp.split(xf, 2, axis=-1)
    c = cos[:, :, None, :]
    s = sin[:, :, None, :]
    return jnp.concatenate([x1 * c - x2 * s, x2 * c + x1 * s], axis=-1).astype(x.dtype)


def diff_attention(h, w_in, w_out, lam_params, subln_g, cos, sin, layer_idx):
    b, s, _ = h.shape
    nblk = s // BLOCK
    qkv = h @ w_in
    nq = 2 * A_HEADS * A_HEAD_DIM
    q, k, v = jnp.split(qkv, [nq, 2 * nq], axis=-1)
    q = apply_rope(q.reshape(b, s, 2 * A_HEADS, A_HEAD_DIM), cos, sin).reshape(b, s, A_HEADS, 2, A_HEAD_DIM)
    k = apply_rope(k.reshape(b, s, 2 * A_HEADS, A_HEAD_DIM), cos, sin).reshape(b, s, A_HEADS, 2, A_HEAD_DIM)
    v = v.reshape(b, s, A_HEADS, A_V_DIM)
    lam_init = 0.8 - 0.6 * math.exp(-0.3 * layer_idx)
    lp = lam_params.astype(jnp.float32)
    lam = jnp.exp(jnp.sum(lp[0] * lp[1])) - jnp.exp(jnp.sum(lp[2] * lp[3])) + lam_init
    scale = A_HEAD_DIM ** -0.5
    kpos = jnp.arange(s)
    q_blocks = jnp.moveaxis(q.reshape(b, nblk, BLOCK, A_HEADS, 2, A_HEAD_DIM), 1, 0)

    def block(args):
        qb, i = args
        sc = jnp.einsum('bqhcd,bkhcd->bhcqk', qb, k).astype(jnp.float32) * scale
        qpos = i * BLOCK + jnp.arange(BLOCK)
        causal = kpos[None, :] <= qpos[:, None]
        p = jax.nn.softmax(jnp.where(causal, sc, -jnp.inf), axis=-1)
        wts = p[:, :, 0] - lam * p[:, :, 1]
        return jnp.einsum('bhqk,bkhe->bqhe', wts.astype(v.dtype), v)

    o = lax.map(block, (q_blocks, jnp.arange(nblk)))
    o = jnp.moveaxis(o, 0, 1).reshape(b, s, A_HEADS, A_V_DIM)
    o = rms_norm(o, subln_g) * (1.0 - lam_init)
    return o.reshape(b, s, A_HEADS * A_V_DIM) @ w_out


def stick_breaking_attention(h, w_in, w_out):
    b, s, _ = h.shape
    nblk = s // BLOCK
    qkv = h @ w_in
    q, k, v = jnp.split(qkv, 3, axis=-1)
    q = q.reshape(b, s, B_HEADS, B_HEAD_DIM)
    k = k.reshape(b, s, B_HEADS, B_HEAD_DIM)
    v = v.reshape(b, s, B_HEADS, B_HEAD_DIM)
    scale = B_HEAD_DIM ** -0.5
    kpos = jnp.arange(s)
    q_blocks = jnp.moveaxis(q.reshape(b, nblk, BLOCK, B_HEADS, B_HEAD_DIM), 1, 0)

    def block(args):
        qb, i = args
        z = jnp.einsum('bqhd,bkhd->bhqk', qb, k).astype(jnp.float32) * scale
        qpos = i * BLOCK + jnp.arange(BLOCK)
        strict = kpos[None, :] < qpos[:, None]
        log_beta = jax.nn.log_sigmoid(z)
        log_1m_beta = jnp.where(strict, jax.nn.log_sigmoid(-z), 0.0)
        suffix = lax.cumsum(log_1m_beta, axis=3, reverse=True) - log_1m_beta
        a = jnp.where(strict, jnp.exp(log_beta + suffix), 0.0)
        return jnp.einsum('bhqk,bkhd->bqhd', a.astype(v.dtype), v)

    o = lax.map(block, (q_blocks, jnp.arange(nblk)))
    o = jnp.moveaxis(o, 0, 1).reshape(b, s, B_HEADS * B_HEAD_DIM)
    return o @ w_out


def sliding_window_sink_attention(h, w_in, w_out, sinks, cos, sin):
    b, s, _ = h.shape
    nblk = s // WINDOW
    qkv = h @ w_in
    nq = C_HEADS * C_HEAD_DIM
    nkv = C_KV_HEADS * C_HEAD_DIM
    q, k, v = jnp.split(qkv, [nq, nq + nkv], axis=-1)
    q = apply_rope(q.reshape(b, s, C_HEADS, C_HEAD_DIM), cos, sin)
    k = apply_rope(k.reshape(b, s, C_KV_HEADS, C_HEAD_DIM), cos, sin)
    v = v.reshape(b, s, C_KV_HEADS, C_HEAD_DIM)
    qb = q.reshape(b, nblk, WINDOW, C_KV_HEADS, C_GROUP, C_HEAD_DIM)
    kb = k.reshape(b, nblk, WINDOW, C_KV_HEADS, C_HEAD_DIM)
    vb = v.reshape(b, nblk, WINDOW, C_KV_HEADS, C_HEAD_DIM)
    pad = ((0, 0), (1, 0), (0, 0), (0, 0), (0, 0))
    kwin = jnp.concatenate([jnp.pad(kb[:, :-1], pad), kb], axis=2)
    vwin = jnp.concatenate([jnp.pad(vb[:, :-1], pad), vb], axis=2)
    sc = jnp.einsum('bnqcgd,bnkcd->bncgqk', qb, kwin).astype(jnp.float32) * (C_HEAD_DIM ** -0.5)
    qi = jnp.arange(WINDOW)[:, None]
    ki = jnp.arange(2 * WINDOW)[None, :]
    rel = WINDOW + qi - ki
    band = (rel >= 0) & (rel < WINDOW)
    in_seq = (jnp.arange(nblk)[:, None] * WINDOW + jnp.arange(2 * WINDOW)[None, :] - WINDOW) >= 0
    mask = band[None] & in_seq[:, None, :]
    sc = jnp.where(mask[None, :, None, None], sc, -jnp.inf)
    sink = sinks.astype(jnp.float32).reshape(C_KV_HEADS, C_GROUP)[None, None, :, :, None, None]
    m = jnp.maximum(jnp.max(sc, axis=-1, keepdims=True), sink)
    p = jnp.exp(sc - m)
    probs = p / (jnp.sum(p, axis=-1, keepdims=True) + jnp.exp(sink - m))
    o = jnp.einsum('bncgqk,bnkcd->bnqcgd', probs.astype(v.dtype), vwin)
    return o.reshape(b, s, C_HEADS * C_HEAD_DIM) @ w_out


def swiglu(h, w_gate, w_up, w_down):
    return (jax.nn.silu(h @ w_gate) * (h @ w_up)) @ w_down


def setup_inputs(seed: int = 0) -> dict:
    key = jax.random.key(seed)
    ks = jax.random.split(key, 16)
    f32 = jnp.float32

    def dense(k, shape, fan_in):
        return jax.random.normal(k, shape, f32) * (fan_in ** -0.5)

    x = jax.random.normal(ks[0], (BATCH, SEQ, D_MODEL), f32)
    positions = jnp.broadcast_to(jnp.arange(SEQ, dtype=jnp.int32)[None, :], (BATCH, SEQ))
    norm_gains = 1.0 + 0.02 * jax.random.normal(ks[1], (DEPTH, 4, D_MODEL), f32)
    a_w_in = dense(ks[2], (N_A, D_MODEL, A_IN), D_MODEL)
    a_w_out = dense(ks[3], (N_A, A_HEADS * A_V_DIM, D_MODEL), A_HEADS * A_V_DIM)
    a_lambda = 0.1 * jax.random.normal(ks[4], (N_A, 4, A_HEAD_DIM), f32)
    a_subln = 1.0 + 0.02 * jax.random.normal(ks[5], (N_A, A_V_DIM), f32)
    b_w_in = dense(ks[6], (N_B, D_MODEL, B_IN), D_MODEL)
    b_w_out = dense(ks[7], (N_B, B_HEADS * B_HEAD_DIM, D_MODEL), B_HEADS * B_HEAD_DIM)
    c_w_in = dense(ks[8], (N_C, D_MODEL, C_IN), D_MODEL)
    c_w_out = dense(ks[9], (N_C, C_HEADS * C_HEAD_DIM, D_MODEL), C_HEADS * C_HEAD_DIM)
    c_sinks = 0.5 * jax.random.normal(ks[10], (N_C, C_HEADS), f32)
    ffn_w_gate = dense(ks[11], (DEPTH, D_MODEL, FFN_HIDDEN), D_MODEL)
    ffn_w_up = dense(ks[12], (DEPTH, D_MODEL, FFN_HIDDEN), D_MODEL)
    ffn_w_down = dense(ks[13], (DEPTH, FFN_HIDDEN, D_MODEL), FFN_HIDDEN)
    return {"x": x, "positions": positions, "norm_gains": norm_gains,
            "a_w_in": a_w_in, "a_w_out": a_w_out, "a_lambda": a_lambda, "a_subln": a_subln,
            "b_w_in": b_w_in, "b_w_out": b_w_out,
            "c_w_in": c_w_in, "c_w_out": c_w_out, "c_sinks": c_sinks,
            "ffn_w_gate": ffn_w_gate, "ffn_w_up": ffn_w_up, "ffn_w_down": ffn_w_down}


def reference(x, positions, norm_gains, a_w_in, a_w_out, a_lambda, a_subln,
              b_w_in, b_w_out, c_w_in, c_w_out, c_sinks,
              ffn_w_gate, ffn_w_up, ffn_w_down):
    cos, sin = rope_tables(positions)
    for i in range(DEPTH):
        kind = i % N_MIXERS
        inst = i // N_MIXERS
        hn = rms_norm(x, norm_gains[i, 0])
        if kind == 0:
            m = diff_attention(hn, a_w_in[inst], a_w_out[inst], a_lambda[inst], a_subln[inst], cos, sin, i)
        elif kind == 1:
            m = stick_breaking_attention(hn, b_w_in[inst], b_w_out[inst])
        else:
            m = sliding_window_sink_attention(hn, c_w_in[inst], c_w_out[inst], c_sinks[inst], cos, sin)
        x = x + rms_norm(m, norm_gains[i, 1])
        f = swiglu(rms_norm(x, norm_gains[i, 2]), ffn_w_gate[i], ffn_w_up[i], ffn_w_down[i])
        x = x + rms_norm(f, norm_gains[i, 3])
    return x
```

```python
from contextlib import ExitStack
import math
import numpy as np
import concourse.bass as bass
import concourse.mybir as mybir
from concourse.bass_utils import run_bass_kernel_spmd

F32 = mybir.dt.float32
BF16 = mybir.dt.bfloat16
I32 = mybir.dt.int32
ALU = mybir.AluOpType
AF = mybir.ActivationFunctionType
AX = mybir.AxisListType


class Buf:
    __slots__ = ("name", "last_w", "readers", "psum")

    def __init__(self, name="", psum=False):
        self.name = name
        self.psum = psum
        self.last_w = None
        self.readers = []


class _Op:
    __slots__ = ("eng", "fn", "deps", "cnt", "dma_key", "idx", "has_dep")


class Sched:
    def __init__(self):
        self.ops = []

    def op(self, eng, fn, reads=(), writes=(), dma_key=None):
        o = _Op()
        o.eng = eng
        o.fn = fn
        o.idx = len(self.ops)
        o.dma_key = dma_key
        o.has_dep = False
        o.cnt = 0
        deps = set()
        for b in reads:
            if b.last_w is not None:
                deps.add(b.last_w)
            if b.psum:
                deps.update(r for r in b.readers if self.ops[r].eng != eng)
        for b in writes:
            if b.last_w is not None:
                deps.add(b.last_w)
            deps.update(b.readers)
        for b in reads:
            b.readers.append(o.idx)
        for b in writes:
            b.last_w = o.idx
            b.readers = []
        deps.discard(o.idx)
        o.deps = deps
        self.ops.append(o)
        return o

    def emit(self, nc, es):
        ops = self.ops
        engs = ["pe", "act", "dve", "pool", "sp"]
        for o in ops:
            for d in o.deps:
                p = ops[d]
                if p.dma_key is None and p.eng == "pe" and o.eng == "pe" and o.dma_key is None:
                    continue
                p.has_dep = True
        ecnt = {e: 0 for e in engs}
        kcnt = {}
        for o in ops:
            if o.dma_key is not None:
                kcnt[o.dma_key] = kcnt.get(o.dma_key, 0) + 16
                o.cnt = kcnt[o.dma_key]
            elif o.has_dep:
                ecnt[o.eng] += 1
                o.cnt = ecnt[o.eng]
        sems = {}
        for e in engs:
            sems[("e", e)] = es.enter_context(nc.semaphore("sem_" + e))
        for k in kcnt:
            sems[("k", k)] = es.enter_context(nc.semaphore("semk_" + str(k)))
        self.nsems = len(sems)
        block = es.enter_context(nc.Block())
        known = {e: {} for e in engs}

        def run(ename, eng):
            kn = known[ename]
            for o in ops:
                if o.eng != ename:
                    continue
                need = {}
                for d in o.deps:
                    p = ops[d]
                    if p.dma_key is not None:
                        key = ("k", p.dma_key)
                    else:
                        if p.eng == "pe" and ename == "pe" and o.dma_key is None:
                            continue
                        key = ("e", p.eng)
                    if p.cnt > need.get(key, 0):
                        need[key] = p.cnt
                for key, v in need.items():
                    if kn.get(key, 0) >= v:
                        continue
                    eng.wait_ge(sems[key], v)
                    kn[key] = v
                ins = o.fn(eng)
                if ins is None:
                    continue
                if o.dma_key is not None:
                    ins.then_inc(sems[("k", o.dma_key)], 16)
                elif o.has_dep:
                    ins.then_inc(sems[("e", ename)], 1)

        @block.tensor
        def _(e):
            run("pe", e)

        @block.scalar
        def _(e):
            run("act", e)

        @block.vector
        def _(e):
            run("dve", e)

        @block.gpsimd
        def _(e):
            run("pool", e)

        @block.sync
        def _(e):
            run("sp", e)


D = 1024
HID = 2816
NHC = HID // 128
EPS = 1e-6
NT = 512


STAGE=[99]
def tail_program(T):
    nc = bass.Bass("TRN2", target_bir_lowering=False)
    xT = nc.dram_tensor("xT", [D, T], F32, kind="ExternalInput").ap()
    oT = nc.dram_tensor("oT", [D, T], F32, kind="ExternalInput").ap()
    w_out = nc.dram_tensor("w_out", [D, D], F32, kind="ExternalInput").ap()
    w_g = nc.dram_tensor("w_g", [D, HID], F32, kind="ExternalInput").ap()
    w_u = nc.dram_tensor("w_u", [D, HID], F32, kind="ExternalInput").ap()
    w_d = nc.dram_tensor("w_d", [HID, D], F32, kind="ExternalInput").ap()
    gains = nc.dram_tensor("gains", [128, 24], F32, kind="ExternalInput").ap()
    yT = nc.dram_tensor("yT", [D, T], F32, kind="ExternalOutput").ap()
    with ExitStack() as es:
        emit_tail(nc, es, T, xT, oT, w_out, w_g, w_u, w_d, gains, yT)
    return nc


def emit_tail(nc, es, T, xT, oT, w_out, w_g, w_u, w_d, gains, yT):
    sb = lambda name, shape, dt: es.enter_context(nc.sbuf_tensor(name, shape, dt))
    ps = lambda name: es.enter_context(nc.psum_tensor(name, [128, NT], F32))
    S = Sched()
    ntile = T // NT

    wo_bf = sb("wo_bf", [128, 8, D], BF16)
    g_sb = sb("g_sb", [128, 24], F32)
    ones_bf = sb("ones_bf", [128, 128], BF16)
    x_sb = [sb(f"x_sb{i}", [128, 8, NT], F32) for i in range(2)]
    o_bf = [sb(f"o_bf{i}", [128, 8, NT], BF16) for i in range(2)]
    m_sb = sb("m_sb", [128, 8, NT], F32)
    sq = [sb(f"sq{i}", [128, NT], BF16) for i in range(2)]
    lnv = sb("lnv", [128, NT], F32)
    rstd = sb("rstd", [128, NT], F32)
    tmp = [sb(f"tmp{i}", [128, NT], F32) for i in range(2)]
    hn = sb("hn", [128, 8, NT], BF16)
    h = sb("h", [128, NHC, NT], BF16)
    sl = [sb(f"sl{i}", [128, NT], F32) for i in range(2)]
    wg_b = [sb(f"wg_b{i}", [128, 8, 512], BF16) for i in range(2)]
    wu_b = [sb(f"wu_b{i}", [128, 8, 512], BF16) for i in range(2)]
    wd_b = [sb(f"wd_b{i}", [128, NHC, 512], BF16) for i in range(2)]
    pm = [ps(f"pm{i}") for i in range(2)]
    pss = ps("pss")
    pg = [ps(f"pg{i}") for i in range(2)]
    pu = [ps(f"pu{i}") for i in range(2)]

    B = lambda n: Buf(n)
    b_wo, b_g, b_ones = B("wo"), B("g"), B("ones")
    b_x = [[B("x") for _ in range(8)] for _ in range(2)]
    b_o = [B("o0"), B("o1")]
    b_m = [B("m") for _ in range(8)]
    b_sq = [B("sq0"), B("sq1")]
    b_lnv, b_rstd = B("lnv"), B("rstd")
    b_tmp = [B("t0"), B("t1")]
    b_hn = [B("hn") for _ in range(8)]
    b_h = [B("h") for _ in range(NHC)]
    b_sl = [B("sl0"), B("sl1")]
    b_wg = [B("wg0"), B("wg1")]
    b_wu = [B("wu0"), B("wu1")]
    b_wd = [B("wd0"), B("wd1")]
    P = lambda n: Buf(n, psum=True)
    b_pm = [P("pm0"), P("pm1")]
    b_pss = P("pss")
    b_pg = [P("pg0"), P("pg1")]
    b_pu = [P("pu0"), P("pu1")]
    b_y = [B("y0"), B("y1")]

    xv = xT.rearrange("(c p) t -> p c t", p=128)
    ov = oT.rearrange("(c p) t -> p c t", p=128)
    yv = yT.rearrange("(c p) t -> p c t", p=128)
    wov = w_out.rearrange("(c p) n -> p c n", p=128)
    wgv = w_g.rearrange("(c p) n -> p c n", p=128)
    wuv = w_u.rearrange("(c p) n -> p c n", p=128)
    wdv = w_d.rearrange("(c p) n -> p c n", p=128)

    S.op("sp", lambda e: e.dma_start(out=g_sb[:], in_=gains), writes=[b_g], dma_key="misc")
    S.op("dve", lambda e: e.memset(ones_bf[:], 1.0), writes=[b_ones])
    for half in range(2):
        S.op("pool", lambda e, half=half: e.dma_start(out=wo_bf[:, :, half * 512:(half + 1) * 512],
                                                      in_=wov[:, :, half * 512:(half + 1) * 512]),
             writes=[b_wo], dma_key="wo")

    groups = []
    c0 = 0
    while c0 < NHC:
        n = min(4, NHC - c0)
        groups.append((c0, n))
        c0 += n

    gcount = [0]

    def load_group(gi):
        c0, n = groups[gi]
        s = gcount[0] % 2
        gcount[0] += 1
        S.op("pool", lambda e: e.dma_start(out=wg_b[s][:, :, 0:n * 128], in_=wgv[:, :, c0 * 128:(c0 + n) * 128]),
             writes=[b_wg[s]], dma_key=f"wg{s}")
        S.op("pool", lambda e: e.dma_start(out=wu_b[s][:, :, 0:n * 128], in_=wuv[:, :, c0 * 128:(c0 + n) * 128]),
             writes=[b_wu[s]], dma_key=f"wu{s}")
        return s

    def load_wd(half):
        for (a, b) in ((0, 8), (8, 16), (16, NHC)):
            S.op("pool", lambda e, a=a, b=b: e.dma_start(out=wd_b[half][:, a:b, :],
                                                         in_=wdv[:, a:b, half * 512:(half + 1) * 512]),
                 writes=[b_wd[half]], dma_key=f"wd{half}")

    def load_tile(tt):
        s = tt % 2
        t0 = tt * NT
        S.op("sp", lambda e: e.dma_start(out=x_sb[s][:], in_=xv[:, :, t0:t0 + NT]),
             writes=b_x[s], dma_key=f"x{s}")
        S.op("pool", lambda e: e.dma_start(out=o_bf[s][:], in_=ov[:, :, t0:t0 + NT]),
             writes=[b_o[s]], dma_key=f"o{s}")

    def norm_stats_from(src_ap_fn, src_bufs_fn, oc):
        q = oc % 2
        S.op("act", lambda e: e.activation(out=sq[q][:], in_=src_ap_fn(oc), func=AF.Square),
             reads=src_bufs_fn(oc), writes=[b_sq[q]])
        S.op("pe", lambda e: e.matmul(pss[:], lhsT=ones_bf[:], rhs=sq[q][:], start=(oc == 0), stop=(oc == 7)),
             reads=[b_ones, b_sq[q]], writes=[b_pss])

    def make_rstd():
        S.op("act", lambda e: e.activation(out=lnv[:], in_=pss[:], func=AF.Ln, bias=EPS, scale=1.0 / D),
             reads=[b_pss], writes=[b_lnv])
        S.op("act", lambda e: e.activation(out=rstd[:], in_=lnv[:], func=AF.Exp, scale=-0.5),
             reads=[b_lnv], writes=[b_rstd])

    def residual_add(s, gj):
        for oc in range(8):
            q = oc % 2
            S.op("dve", lambda e, oc=oc, q=q: e.scalar_tensor_tensor(
                out=tmp[q][:], in0=m_sb[:, oc, :], scalar=g_sb[:, gj * 8 + oc:gj * 8 + oc + 1], in1=rstd[:],
                op0=ALU.mult, op1=ALU.mult), reads=[b_m[oc], b_g, b_rstd], writes=[b_tmp[q]])
            S.op("dve", lambda e, oc=oc, q=q: e.tensor_tensor(
                out=x_sb[s][:, oc, :], in0=x_sb[s][:, oc, :], in1=tmp[q][:], op=ALU.add),
                reads=[b_tmp[q], b_x[s][oc]], writes=[b_x[s][oc]])

    load_tile(0)

    def tile_body(tt):
        s = tt % 2
        if tt + 1 < ntile:
            load_tile(tt + 1)
        for oc in range(8):
            p = oc % 2
            for kc in range(8):
                S.op("pe", lambda e, oc=oc, kc=kc, p=p: e.matmul(
                    pm[p][:], lhsT=wo_bf[:, kc, oc * 128:(oc + 1) * 128], rhs=o_bf[s][:, kc, :],
                    start=(kc == 0), stop=(kc == 7)), reads=[b_wo, b_o[s]], writes=[b_pm[p]])
            S.op("dve", lambda e, oc=oc, p=p: e.tensor_copy(out=m_sb[:, oc, :], in_=pm[p][:]),
                 reads=[b_pm[p]], writes=[b_m[oc]])
            norm_stats_from(lambda oc_, p=p: pm[p][:], lambda oc_, p=p: [b_pm[p]], oc)
        if STAGE[0] >= 2:
            make_rstd()
        if STAGE[0] >= 3:
            residual_add(s, 0)
        if STAGE[0] < 4:
            t0 = tt * NT
            S.op("sp", lambda e, s=s, t0=t0: e.dma_start(out=yv[:, :, t0:t0 + NT], in_=x_sb[s][:]),
                 reads=b_x[s] + b_m + [b_rstd], writes=[b_y[s]], dma_key=f"st{s}")
            return
        for oc in range(8):
            norm_stats_from(lambda oc_: x_sb[s][:, oc_, :], lambda oc_: [b_x[s][oc_]], oc)
        make_rstd()
        for oc in range(8):
            S.op("dve", lambda e, oc=oc: e.scalar_tensor_tensor(
                out=hn[:, oc, :], in0=x_sb[s][:, oc, :], scalar=g_sb[:, 8 + oc:8 + oc + 1], in1=rstd[:],
                op0=ALU.mult, op1=ALU.mult), reads=[b_x[s][oc], b_g, b_rstd], writes=[b_hn[oc]])
        slot = load_group(0)
        for gi, (c0, n) in enumerate(groups):
            cur = slot
            if gi + 1 < len(groups):
                slot = load_group(gi + 1)
            elif True:
                load_wd(0)
                load_wd(1)
            for j in range(n):
                hc = c0 + j
                p = hc % 2
                for kc in range(8):
                    S.op("pe", lambda e, j=j, kc=kc, p=p, cur=cur: e.matmul(
                        pg[p][:], lhsT=wg_b[cur][:, kc, j * 128:(j + 1) * 128], rhs=hn[:, kc, :],
                        start=(kc == 0), stop=(kc == 7)), reads=[b_wg[cur], b_hn[kc]], writes=[b_pg[p]])
                for kc in range(8):
                    S.op("pe", lambda e, j=j, kc=kc, p=p, cur=cur: e.matmul(
                        pu[p][:], lhsT=wu_b[cur][:, kc, j * 128:(j + 1) * 128], rhs=hn[:, kc, :],
                        start=(kc == 0), stop=(kc == 7)), reads=[b_wu[cur], b_hn[kc]], writes=[b_pu[p]])
                S.op("act", lambda e, p=p: e.activation(out=sl[p][:], in_=pg[p][:], func=AF.Silu),
                     reads=[b_pg[p]], writes=[b_sl[p]])
                S.op("dve", lambda e, p=p, hc=hc: e.tensor_tensor(
                    out=h[:, hc, :], in0=sl[p][:], in1=pu[p][:], op=ALU.mult),
                    reads=[b_sl[p], b_pu[p]], writes=[b_h[hc]])
        for oc in range(8):
            p = oc % 2
            half, j = oc // 4, oc % 4
            for hc in range(NHC):
                S.op("pe", lambda e, hc=hc, p=p, half=half, j=j: e.matmul(
                    pm[p][:], lhsT=wd_b[half][:, hc, j * 128:(j + 1) * 128], rhs=h[:, hc, :],
                    start=(hc == 0), stop=(hc == NHC - 1)), reads=[b_wd[half], b_h[hc]], writes=[b_pm[p]])
            S.op("dve", lambda e, oc=oc, p=p: e.tensor_copy(out=m_sb[:, oc, :], in_=pm[p][:]),
                 reads=[b_pm[p]], writes=[b_m[oc]])
            norm_stats_from(lambda oc_, p=p: pm[p][:], lambda oc_, p=p: [b_pm[p]], oc)
        make_rstd()
        residual_add(s, 2)
        t0 = tt * NT
        S.op("sp", lambda e, s=s, t0=t0: e.dma_start(out=yv[:, :, t0:t0 + NT], in_=x_sb[s][:]),
             reads=b_x[s], writes=[b_y[s]], dma_key=f"st{s}")
    for tt in range(ntile):
        tile_body(tt)
    S.op("sp", lambda e: None, reads=b_y)
    S.emit(nc, es)
    return S


D = 1024
EPS = 1e-6
NT = 512
PI = math.pi


def attn_program(mode, S):
    nc = bass.Bass("TRN2", target_bir_lowering=False)
    KW = 64 if mode == "C" else 512
    dr = lambda n, sh, dt=F32, kind="ExternalInput": nc.dram_tensor(n, sh, dt, kind=kind).ap()
    io = dict(
        xT=dr("xT", [D, S]), g0=dr("g0", [128, 8]), wq=dr("wq", [D, 512]), wk=dr("wk", [D, KW]),
        wv=dr("wv", [D, KW]), cmat=dr("cmat", [128, 256]), rc=dr("rc", [128, 4]),
        pos=dr("pos", [128, S], I32), extra=dr("extra", [128, 264]),
        oT=dr("oT", [512, S], F32, kind="ExternalOutput"),
    )
    with ExitStack() as es:
        emit_attn(nc, es, mode, S, io)
    return nc


def emit_attn(nc, es, mode, S, io):
    sb = lambda name, shape, dt: es.enter_context(nc.sbuf_tensor(name, shape, dt))
    Sx = Sched()
    op = Sx.op
    ntile = S // NT
    nkb = S // 128
    KW = 64 if mode == "C" else 512
    rope = mode in ("A", "C")
    xv = io["xT"].rearrange("(c p) t -> p c t", p=128)
    oT = io["oT"]

    bank = [es.enter_context(nc.psum_tensor(f"bank{i}", [128, NT], F32)) for i in range(8)]
    b_bank = [Buf(f"bank{i}", psum=True) for i in range(8)]

    g_sb = sb("g_sb", [128, 8], F32)
    rc_sb = sb("rc_sb", [128, 4], F32)
    ex_sb = sb("ex_sb", [128, 264], F32)
    cm_bf = sb("cm_bf", [128, 256], BF16)
    ones_bf = sb("ones_bf", [128, 128], BF16)
    wq_bf = sb("wq_bf", [128, 8, 512], BF16)
    KWk = 128 if mode == "C" else 512
    wk_bf = sb("wk_bf", [128, 8, KWk], BF16)
    wv_bf = sb("wv_bf", [128, 8, KW], BF16)
    b_g, b_rc, b_ex, b_cm, b_ones, b_wq, b_wk, b_wv = [Buf(n) for n in "g rc ex cm ones wq wk wv".split()]
    op("sp", lambda e: e.dma_start(out=g_sb[:], in_=io["g0"]), writes=[b_g], dma_key="c0")
    op("sp", lambda e: e.dma_start(out=rc_sb[:], in_=io["rc"]), writes=[b_rc], dma_key="c1")
    op("sp", lambda e: e.dma_start(out=ex_sb[:], in_=io["extra"]), writes=[b_ex], dma_key="c2")
    op("pool", lambda e: e.dma_start(out=cm_bf[:], in_=io["cmat"]), writes=[b_cm], dma_key="c3")
    op("dve", lambda e: e.memset(ones_bf[:], 1.0), writes=[b_ones])
    wqv = io["wq"].rearrange("(c p) n -> p c n", p=128)
    wkv = io["wk"].rearrange("(c p) n -> p c n", p=128)
    wvv = io["wv"].rearrange("(c p) n -> p c n", p=128)
    op("pool", lambda e: e.dma_start(out=wq_bf[:], in_=wqv), writes=[b_wq], dma_key="w0")
    if mode == "C":
        op("pool", lambda e: e.dma_start(out=wk_bf[:, :, 0:64], in_=wkv), writes=[b_wk], dma_key="w1")
        op("pool", lambda e: e.dma_start(out=wk_bf[:, :, 64:128], in_=wkv), writes=[b_wk], dma_key="w1")
    else:
        op("pool", lambda e: e.dma_start(out=wk_bf[:], in_=wkv), writes=[b_wk], dma_key="w1")
    op("pool", lambda e: e.dma_start(out=wv_bf[:], in_=wvv), writes=[b_wv], dma_key="w2")

    if rope:
        Ctab = sb("Ctab", [128, NT], F32)
        Stab = sb("Stab", [128, NT], F32)
        pos_i = sb("pos_i", [128, NT], I32)
        ang = sb("ang", [128, NT], F32)
        tr = sb("tr", [128, NT], F32)
        tr2 = sb("tr2", [128, NT], F32)
        b_tr2 = Buf("tr2")
        b_C = [Buf("C")] * ntile
        b_S = [Buf("S")] * ntile
        b_pos, b_ang, b_tr = Buf("pos"), Buf("ang"), Buf("tr")
        negpi = sb("negpi", [128, 1], F32)
        b_negpi = Buf("negpi")
        op("dve", lambda e: e.memset(negpi[:], -PI), writes=[b_negpi])
    def rope_tables(tt):
        if True:
            sl_ = slice(tt * NT, (tt + 1) * NT)
            op("sp", lambda e, sl_=sl_: e.dma_start(out=pos_i[:], in_=io["pos"][:, sl_]), writes=[b_pos], dma_key="pos")
            op("dve", lambda e: e.tensor_copy(out=ang[:], in_=pos_i[:]), reads=[b_pos], writes=[b_ang])
            op("dve", lambda e: e.tensor_scalar(out=ang[:], in0=ang[:], scalar1=rc_sb[:, 0:1], scalar2=None, op0=ALU.mult),
               reads=[b_ang, b_rc], writes=[b_ang])
            for (tab, b_tab, shift, sgn) in ((Stab, b_S, 0.0, True), (Ctab, b_C, 0.5 * PI, False)):
                op("dve", lambda e, shift=shift: e.tensor_scalar(out=tr[:], in0=ang[:], scalar1=shift, scalar2=1.0 / (2 * PI),
                                                                 op0=ALU.add, op1=ALU.mult), reads=[b_ang], writes=[b_tr])
                op("dve", lambda e: e.tensor_copy(out=pos_i[:], in_=tr[:]), reads=[b_tr], writes=[b_pos])
                op("dve", lambda e: e.tensor_copy(out=tr[:], in_=pos_i[:]), reads=[b_pos], writes=[b_tr])
                op("dve", lambda e: e.tensor_scalar(out=tr[:], in0=tr[:], scalar1=-2 * PI, scalar2=None, op0=ALU.mult),
                   reads=[b_tr], writes=[b_tr])
                op("dve", lambda e, shift=shift: e.scalar_tensor_tensor(out=tr[:], in0=ang[:], scalar=shift, in1=tr[:],
                                                                        op0=ALU.add, op1=ALU.add), reads=[b_ang, b_tr], writes=[b_tr])
                op("dve", lambda e: e.tensor_scalar(out=tr2[:], in0=tr[:], scalar1=PI, scalar2=-2 * PI, op0=ALU.is_gt, op1=ALU.mult),
                   reads=[b_tr], writes=[b_tr2])
                op("dve", lambda e: e.tensor_tensor(out=tr[:], in0=tr[:], in1=tr2[:], op=ALU.add), reads=[b_tr, b_tr2], writes=[b_tr])
                op("act", lambda e, sl_=sl_, tab=tab: e.activation(out=tab[:], in_=tr[:], func=AF.Sin),
                   reads=[b_tr], writes=[b_tab[tt]])
                if sgn:
                    op("dve", lambda e, sl_=sl_, tab=tab: e.tensor_scalar(out=tab[:], in0=tab[:], scalar1=rc_sb[:, 1:2],
                                                                          scalar2=None, op0=ALU.mult),
                       reads=[b_tab[tt], b_rc], writes=[b_tab[tt]])

    nqc = 4
    nkc = 1 if mode == "C" else 4
    qT = sb("qT", [128, nqc, S], BF16)
    kT = sb("kT", [128, nkc, S], BF16)
    v_sb = sb("v_sb", [128, nkb, KW], BF16)
    b_q = [[Buf("q") for _ in range(ntile)] for _ in range(nqc)]
    b_k = [[Buf("k") for _ in range(ntile)] for _ in range(nkc)]
    b_v = [Buf("v") for _ in range(nkb)]
    x_sb = sb("x_sb", [128, 8, NT], F32)
    hn = sb("hn", [128, 8, NT], BF16)
    sq = [sb(f"sq{i}", [128, NT], BF16) for i in range(2)]
    lnv = sb("lnv", [128, NT], F32)
    rstd = sb("rstd", [128, NT], F32)
    qb = [sb(f"qb{i}", [128, NT], BF16) for i in range(2)]
    t1 = [sb(f"t1{i}", [128, NT], F32) for i in range(2)]
    t2 = [sb(f"t2{i}", [128, NT], F32) for i in range(2)]
    b_x = [Buf("x") for _ in range(8)]
    b_hn = [Buf("hn") for _ in range(8)]
    b_sq = [Buf("sq0"), Buf("sq1")]
    b_lnv, b_rstd = Buf("lnv"), Buf("rstd")
    b_qb = [Buf("qb0"), Buf("qb1")]
    b_t1 = [Buf("t10"), Buf("t11")]
    b_t2 = [Buf("t20"), Buf("t21")]
    PP = [0, 1]
    PSW = [2, 3]
    PSS = 4
    cnt = [0]

    def proj_feat(tt, w_bf, b_w, c, npart, dstT, b_dst, do_rope, scale):
        sl_ = slice(tt * NT, (tt + 1) * NT)
        i = cnt[0] % 2
        cnt[0] += 1
        pb, sw = PP[i], PSW[i]
        for kc in range(8):
            op("pe", lambda e, kc=kc: e.matmul(bank[pb][0:npart, :], lhsT=w_bf[:, kc, c * 128:c * 128 + npart], rhs=hn[:, kc, :],
                                               start=(kc == 0), stop=(kc == 7)), reads=[b_w, b_hn[kc]], writes=[b_bank[pb]])
        if not do_rope:
            if scale == 1.0:
                op("act", lambda e: e.copy(out=dstT[0:npart, c, sl_], in_=bank[pb][0:npart, :]), reads=[b_bank[pb]], writes=[b_dst])
            else:
                op("dve", lambda e: e.tensor_scalar(out=dstT[0:npart, c, sl_], in0=bank[pb][0:npart, :], scalar1=scale, scalar2=None,
                                                    op0=ALU.mult), reads=[b_bank[pb]], writes=[b_dst])
            return
        op("act", lambda e: e.copy(out=qb[i][0:npart, :], in_=bank[pb][0:npart, :]), reads=[b_bank[pb]], writes=[b_qb[i]])
        op("pe", lambda e: e.matmul(bank[sw][0:npart, :], lhsT=cm_bf[0:npart, 0:npart], rhs=qb[i][0:npart, :], start=True, stop=True),
           reads=[b_cm, b_qb[i]], writes=[b_bank[sw]])
        op("dve", lambda e: e.tensor_tensor(out=t1[i][0:npart, :], in0=bank[pb][0:npart, :], in1=Ctab[0:npart, :], op=ALU.mult),
           reads=[b_bank[pb], b_C[tt]], writes=[b_t1[i]])
        op("dve", lambda e: e.tensor_tensor(out=t2[i][0:npart, :], in0=bank[sw][0:npart, :], in1=Stab[0:npart, :], op=ALU.mult),
           reads=[b_bank[sw], b_S[tt]], writes=[b_t2[i]])
        op("pool", lambda e: e.tensor_tensor(out=dstT[0:npart, c, sl_], in0=t1[i][0:npart, :], in1=t2[i][0:npart, :], op=ALU.add),
           reads=[b_t1[i], b_t2[i]], writes=[b_dst])

    def proj_tile(tt):
        sl_ = slice(tt * NT, (tt + 1) * NT)
        if rope:
            rope_tables(tt)
        op("sp", lambda e: e.dma_start(out=x_sb[:], in_=xv[:, :, sl_]), writes=b_x, dma_key="x")
        for kc in range(8):
            q_ = kc % 2
            op("act", lambda e, kc=kc, q_=q_: e.activation(out=sq[q_][:], in_=x_sb[:, kc, :], func=AF.Square),
               reads=[b_x[kc]], writes=[b_sq[q_]])
            op("pe", lambda e, kc=kc, q_=q_: e.matmul(bank[PSS][:], lhsT=ones_bf[:], rhs=sq[q_][:], start=(kc == 0), stop=(kc == 7)),
               reads=[b_ones, b_sq[q_]], writes=[b_bank[PSS]])
        op("act", lambda e: e.activation(out=lnv[:], in_=bank[PSS][:], func=AF.Ln, bias=EPS, scale=1.0 / D),
           reads=[b_bank[PSS]], writes=[b_lnv])
        op("act", lambda e: e.activation(out=rstd[:], in_=lnv[:], func=AF.Exp, scale=-0.5), reads=[b_lnv], writes=[b_rstd])
        for kc in range(8):
            op("dve", lambda e, kc=kc: e.scalar_tensor_tensor(out=hn[:, kc, :], in0=x_sb[:, kc, :], scalar=g_sb[:, kc:kc + 1], in1=rstd[:],
                                                             op0=ALU.mult, op1=ALU.mult),
               reads=[b_x[kc], b_g, b_rstd], writes=[b_hn[kc]])
        for c in range(nqc):
            proj_feat(tt, wq_bf, b_wq, c, 128, qT, b_q[c][tt], rope, 1.0)
        for c in range(nkc):
            proj_feat(tt, wk_bf, b_wk, c, 128, kT, b_k[c][tt], rope, 0.125 if mode == "B" else 1.0)
        for sub in range(4):
            i = cnt[0] % 2
            cnt[0] += 1
            pb = PP[i]
            kbi = tt * 4 + sub
            for kc in range(8):
                op("pe", lambda e, kc=kc, sub=sub, pb=pb: e.matmul(bank[pb][:, 0:KW], lhsT=hn[:, kc, sub * 128:(sub + 1) * 128],
                                                                   rhs=wv_bf[:, kc, :], start=(kc == 0), stop=(kc == 7)),
                   reads=[b_wv, b_hn[kc]], writes=[b_bank[pb]])
            op("act", lambda e, kbi=kbi, pb=pb: e.copy(out=v_sb[:, kbi, :], in_=bank[pb][:, 0:KW]), reads=[b_bank[pb]], writes=[b_v[kbi]])

    for tt in range(ntile):
        proj_tile(tt)

    b_out = [Buf("out0"), Buf("out1")]
    if mode == "A":
        lp = ex_sb
        lt = sb("lt", [128, 128], F32)
        ls = sb("ls", [128, 8], F32)
        b_lt, b_ls = Buf("lt"), Buf("ls")
        op("dve", lambda e: e.tensor_tensor(out=lt[:, 0:64], in0=lp[:, 0:64], in1=lp[:, 64:128], op=ALU.mult), reads=[b_ex], writes=[b_lt])
        op("dve", lambda e: e.tensor_tensor(out=lt[:, 64:128], in0=lp[:, 128:192], in1=lp[:, 192:256], op=ALU.mult), reads=[b_ex, b_lt], writes=[b_lt])
        op("dve", lambda e: e.tensor_reduce(out=ls[:, 0:1], in_=lt[:, 0:64], axis=AX.X, op=ALU.add), reads=[b_lt], writes=[b_ls])
        op("dve", lambda e: e.tensor_reduce(out=ls[:, 1:2], in_=lt[:, 64:128], axis=AX.X, op=ALU.add), reads=[b_lt, b_ls], writes=[b_ls])
        op("act", lambda e: e.activation(out=ls[:, 2:4], in_=ls[:, 0:2], func=AF.Exp), reads=[b_ls], writes=[b_ls])
        op("dve", lambda e: e.tensor_tensor(out=ls[:, 4:5], in0=ls[:, 2:3], in1=ls[:, 3:4], op=ALU.subtract), reads=[b_ls], writes=[b_ls])
        op("dve", lambda e: e.tensor_scalar(out=ls[:, 5:6], in0=ls[:, 4:5], scalar1=lp[:, 257:258], scalar2=-1.0, op0=ALU.add, op1=ALU.mult),
           reads=[b_ls, b_ex], writes=[b_ls])
        op("dve", lambda e: e.tensor_tensor(out=ls[:, 6:7], in0=lp[:, 256:257], in1=lp[:, 258:259], op=ALU.mult), reads=[b_ls, b_ex], writes=[b_ls])
        pT = [sb(f"pT{i}", [128, NT], BF16) for i in range(4)]
        b_pT = [Buf(f"pT{i}") for i in range(4)]
        r0 = sb("r0", [128, NT], F32)
        ta = [sb(f"ta{i}", [128, NT], F32) for i in range(2)]
        ob = sb("ob", [128, NT], F32)
        osq = sb("osq", [128, NT], BF16)
        of = [sb(f"of{i}", [128, NT], F32) for i in range(2)]
        b_r0, b_ob, b_osq = Buf("r0"), Buf("ob"), Buf("osq")
        b_ta = [Buf("ta0"), Buf("ta1")]
        b_of = [Buf("of0"), Buf("of1")]
        SC = [0, 1]
        NUM = [2, 3]
        DEN = [5, 6]
        SSB = 7
        pcnt = [0]
        ocnt = [0]
        for h in range(4):
            for qt in range(ntile):
                last = 4 * qt + 3
                for kb in range(last + 1):
                    j = kb - 4 * qt
                    col0 = j * 128 if j > 0 else 0
                    for c in range(2):
                        i = pcnt[0]
                        pcnt[0] += 1
                        sc, r = SC[i % 2], i % 4
                        ps_ = slice(c * 64, (c + 1) * 64)
                        op("pe", lambda e, kb=kb, col0=col0, sc=sc, ps_=ps_, h=h, qt=qt: e.matmul(
                            bank[sc][:, col0:NT], lhsT=kT[ps_, h, kb * 128:(kb + 1) * 128],
                            rhs=qT[ps_, h, qt * NT + col0:(qt + 1) * NT], start=True, stop=True),
                            reads=[b_k[h][kb // 4], b_q[h][qt]], writes=[b_bank[sc]])
                        op("act", lambda e, col0=col0, sc=sc, r=r: e.activation(out=pT[r][:, col0:NT], in_=bank[sc][:, col0:NT],
                                                                              func=AF.Exp, scale=0.125),
                           reads=[b_bank[sc]], writes=[b_pT[r]])
                        if j >= 0:
                            op("pool", lambda e, col0=col0, r=r: e.affine_select(
                                out=pT[r][:, col0:col0 + 128], in_=pT[r][:, col0:col0 + 128], pattern=[[1, 128]],
                                compare_op=ALU.is_ge, fill=0.0, base=0, channel_multiplier=-1),
                                reads=[b_pT[r]], writes=[b_pT[r]])
                        op("pe", lambda e, kb=kb, col0=col0, r=r, c=c, h=h, last=last: e.matmul(
                            bank[NUM[c]][:, col0:NT], lhsT=v_sb[:, kb, h * 128:(h + 1) * 128], rhs=pT[r][:, col0:NT],
                            start=(kb == 0), stop=(kb == last)), reads=[b_v[kb], b_pT[r]], writes=[b_bank[NUM[c]]])
                        op("pe", lambda e, kb=kb, col0=col0, r=r, c=c, last=last: e.matmul(
                            bank[DEN[c]][:, col0:NT], lhsT=ones_bf[:], rhs=pT[r][:, col0:NT],
                            start=(kb == 0), stop=(kb == last)), reads=[b_ones, b_pT[r]], writes=[b_bank[DEN[c]]])
                for c in range(2):
                    op("dve", lambda e, c=c: e.reciprocal(out=r0[:], in_=bank[DEN[c]][:]), reads=[b_bank[DEN[c]]], writes=[b_r0])
                    op("dve", lambda e, c=c: e.tensor_tensor(out=ta[c][:], in0=bank[NUM[c]][:], in1=r0[:], op=ALU.mult),
                       reads=[b_bank[NUM[c]], b_r0], writes=[b_ta[c]])
                op("dve", lambda e: e.scalar_tensor_tensor(out=ob[:], in0=ta[1][:], scalar=ls[:, 5:6], in1=ta[0][:], op0=ALU.mult, op1=ALU.add),
                   reads=[b_ta[0], b_ta[1], b_ls], writes=[b_ob])
                op("act", lambda e: e.activation(out=osq[:], in_=ob[:], func=AF.Square), reads=[b_ob], writes=[b_osq])
                op("pe", lambda e: e.matmul(bank[SSB][:], lhsT=ones_bf[:], rhs=osq[:], start=True, stop=True),
                   reads=[b_ones, b_osq], writes=[b_bank[SSB]])
                op("act", lambda e: e.activation(out=lnv[:], in_=bank[SSB][:], func=AF.Ln, bias=EPS, scale=1.0 / 128),
                   reads=[b_bank[SSB]], writes=[b_lnv])
                op("act", lambda e: e.activation(out=rstd[:], in_=lnv[:], func=AF.Exp, scale=-0.5), reads=[b_lnv], writes=[b_rstd])
                oi = ocnt[0] % 2
                ocnt[0] += 1
                op("dve", lambda e, oi=oi: e.scalar_tensor_tensor(out=of[oi][:], in0=ob[:], scalar=ls[:, 6:7], in1=rstd[:],
                                                                 op0=ALU.mult, op1=ALU.mult),
                   reads=[b_ob, b_ls, b_rstd], writes=[b_of[oi]])
                op("sp", lambda e, oi=oi, h=h, qt=qt: e.dma_start(out=oT[h * 128:(h + 1) * 128, qt * NT:(qt + 1) * NT], in_=of[oi][:]),
                   reads=[b_of[oi]], writes=[b_out[oi]], dma_key=f"out{oi}")
    if mode == "B":
        e_sb = [sb(f"e_sb{i}", [128, NT], F32) for i in range(3)]
        L_sb = [sb(f"L_sb{i}", [128, NT], BF16) for i in range(3)]
        a_sb = [sb(f"a_sb{i}", [128, NT], BF16) for i in range(3)]
        aarg = [sb(f"aarg{i}", [128, NT], F32) for i in range(2)]
        carry = sb("carry", [128, NT], F32)
        ofb = [sb(f"ofb{i}", [64, NT], F32) for i in range(2)]
        b_e = [Buf("e") for _ in range(3)]
        b_L = [Buf("L") for _ in range(3)]
        b_a = [Buf("a") for _ in range(3)]
        b_aarg = [Buf("aarg") for _ in range(2)]
        b_carry = Buf("carry")
        b_ofb = [Buf("ofb0"), Buf("ofb1")]
        ARG = [0, 1, 2]
        CS = [3, 4]
        OPS = [5, 6]
        gq = [0]
        def b_group(h, qt):
            cch, ps_ = h // 2, slice((h % 2) * 64, (h % 2) * 64 + 64)
            blocks = list(range(4 * qt + 3, -1, -1))
            n = len(blocks)
            ob_ = OPS[gq[0] % 2]
            oi = gq[0] % 2
            gq[0] += 1

            def diag_mask(i, tile_, b_t):
                j = blocks[i] - 4 * qt
                if j >= 0:
                    op("pool", lambda e, j=j: e.affine_select(out=tile_[:], in_=tile_[:], pattern=[[1, NT]], compare_op=ALU.is_gt,
                                                              fill=0.0, base=-j * 128, channel_multiplier=-1),
                       reads=[b_t], writes=[b_t])

            def Z(i):
                kb = blocks[i]
                ab = ARG[i % 3]
                op("pe", lambda e: e.matmul(bank[ab][:], lhsT=kT[ps_, cch, kb * 128:(kb + 1) * 128], rhs=qT[ps_, cch, qt * NT:(qt + 1) * NT],
                                            start=True, stop=True), reads=[b_k[cch][kb // 4], b_q[cch][qt]], writes=[b_bank[ab]])

            def EL(i):
                ab, r = ARG[i % 3], i % 3
                op("act", lambda e: e.activation(out=e_sb[r][:], in_=bank[ab][:], func=AF.Exp), reads=[b_bank[ab]], writes=[b_e[r]])
                op("act", lambda e: e.activation(out=L_sb[r][:], in_=e_sb[r][:], func=AF.Ln, bias=1.0), reads=[b_e[r]], writes=[b_L[r]])
                diag_mask(i, L_sb[r], b_L[r])

            def CUM(i):
                ab, r, cb = ARG[i % 3], i % 3, CS[i % 2]
                op("pe", lambda e: e.matmul(bank[ab][:], lhsT=cm_bf[:, 0:128], rhs=L_sb[r][:], start=False, stop=True, skip_group_check=True),
                   reads=[b_cm, b_L[r]], writes=[b_bank[ab]])
                op("pe", lambda e: e.matmul(bank[cb][:], lhsT=ones_bf[:], rhs=L_sb[r][:], start=True, stop=True),
                   reads=[b_ones, b_L[r]], writes=[b_bank[cb]])

            def SUB(i):
                ab, cb, q_ = ARG[i % 3], CS[i % 2], i % 2
                if i == 0:
                    op("dve", lambda e: e.tensor_copy(out=aarg[q_][:], in_=bank[ab][:]), reads=[b_bank[ab]], writes=[b_aarg[q_]])
                    op("dve", lambda e: e.tensor_copy(out=carry[:], in_=bank[cb][:]), reads=[b_bank[cb]], writes=[b_carry])
                else:
                    op("dve", lambda e: e.tensor_tensor(out=aarg[q_][:], in0=bank[ab][:], in1=carry[:], op=ALU.subtract),
                       reads=[b_bank[ab], b_carry], writes=[b_aarg[q_]])
                    op("dve", lambda e: e.tensor_tensor(out=carry[:], in0=bank[cb][:], in1=carry[:], op=ALU.add),
                       reads=[b_bank[cb], b_carry], writes=[b_carry])

            def AA(i):
                r, q_ = i % 3, i % 2
                op("act", lambda e: e.activation(out=a_sb[r][:], in_=aarg[q_][:], func=AF.Exp), reads=[b_aarg[q_]], writes=[b_a[r]])
                diag_mask(i, a_sb[r], b_a[r])

            def PV(i):
                kb, r = blocks[i], i % 3
                op("pe", lambda e: e.matmul(bank[ob_][0:64, :], lhsT=v_sb[:, kb, h * 64:(h + 1) * 64], rhs=a_sb[r][:],
                                            start=(i == 0), stop=(i == n - 1)), reads=[b_v[kb], b_a[r]], writes=[b_bank[ob_]])

            Z(0)
            EL(0)
            if n > 1:
                Z(1)
            for i in range(n):
                CUM(i)
                if i + 1 < n:
                    EL(i + 1)
                if i + 2 < n:
                    Z(i + 2)
                SUB(i)
                AA(i)
                PV(i)
            op("dve", lambda e: e.tensor_copy(out=ofb[oi][:], in_=bank[ob_][0:64, :]), reads=[b_bank[ob_]], writes=[b_ofb[oi]])
            op("sp", lambda e: e.dma_start(out=oT[h * 64:(h + 1) * 64, qt * NT:(qt + 1) * NT], in_=ofb[oi][:]),
               reads=[b_ofb[oi]], writes=[b_out[oi]], dma_key=f"out{oi}")

        for h in range(8):
            for qt in range(ntile):
                b_group(h, qt)

    if mode == "C":
        esk = sb("esk", [128, 8], F32)
        zer = sb("zer", [64, 128], F32)
        esk_t = [sb(f"esk_t{i}", [64, NT], F32) for i in range(2)]
        p_sb = [sb(f"p_sb{i}", [128, NT], BF16) for i in range(4)]
        dn = sb("dn", [64, NT], F32)
        ofc = [sb(f"ofc{i}", [64, NT], F32) for i in range(2)]
        b_esk, b_zer, b_dn = Buf("esk"), Buf("zer"), Buf("dn")
        b_eskt = [Buf("eskt0"), Buf("eskt1")]
        b_p = [Buf(f"p{i}") for i in range(4)]
        b_ofc = [Buf("ofc0"), Buf("ofc1")]
        op("act", lambda e: e.activation(out=esk[:], in_=ex_sb[:, 0:8], func=AF.Exp), reads=[b_ex], writes=[b_esk])
        op("dve", lambda e: e.memset(zer[:], 0.0), writes=[b_zer])
        for hd in range(8):
            op("dve", lambda e, hd=hd: e.tensor_scalar(out=esk_t[hd % 2][:, (hd // 2) * 128:(hd // 2 + 1) * 128], in0=zer[:],
                                                       scalar1=esk[0:64, hd:hd + 1], scalar2=None, op0=ALU.add),
               reads=[b_zer, b_esk, b_eskt[hd % 2]], writes=[b_eskt[hd % 2]])
        oview = oT.rearrange("(h d) s -> d h s", d=64)
        SCB = [0, 1, 2, 3]
        OB = [4, 5]
        DB = [6, 7]
        pc = [0]
        gq = [0]

        def c_block(nb, hg):
            slots = ([(nb - 1, 0)] if nb > 0 else []) + [(nb, 1)]
            pr = []
            for (kb, slot) in slots:
                i = pc[0] % 4
                pc[0] += 1
                pr.append((kb, slot, i))
                for hh in range(4):
                    hd = hh * 2 + hg
                    ps_ = slice((hd % 2) * 64, (hd % 2) * 64 + 64)
                    op("pe", lambda e, hh=hh, hd=hd, ps_=ps_, kb=kb, i=i: e.matmul(
                        bank[SCB[i]][:, hh * 128:(hh + 1) * 128], lhsT=kT[ps_, 0, kb * 128:(kb + 1) * 128],
                        rhs=qT[ps_, hd // 2, nb * 128:(nb + 1) * 128], start=True, stop=True),
                        reads=[b_k[0][kb // 4], b_q[hd // 2][nb // 4]], writes=[b_bank[SCB[i]]])
                op("act", lambda e, i=i: e.activation(out=p_sb[i][:], in_=bank[SCB[i]][:], func=AF.Exp, scale=0.125),
                   reads=[b_bank[SCB[i]]], writes=[b_p[i]])
                for hh in range(4):
                    cs_ = slice(hh * 128, (hh + 1) * 128)
                    if slot == 1:
                        op("pool", lambda e, i=i, cs_=cs_: e.affine_select(out=p_sb[i][:, cs_], in_=p_sb[i][:, cs_], pattern=[[1, 128]],
                                                                          compare_op=ALU.is_ge, fill=0.0, base=0, channel_multiplier=-1),
                           reads=[b_p[i]], writes=[b_p[i]])
                    else:
                        op("pool", lambda e, i=i, cs_=cs_: e.affine_select(out=p_sb[i][:, cs_], in_=p_sb[i][:, cs_], pattern=[[-1, 128]],
                                                                          compare_op=ALU.is_gt, fill=0.0, base=0, channel_multiplier=1),
                           reads=[b_p[i]], writes=[b_p[i]])
            g = gq[0] % 2
            gq[0] += 1
            for hh in range(4):
                cs_ = slice(hh * 128, (hh + 1) * 128)
                for si, (kb, slot, i) in enumerate(pr):
                    op("pe", lambda e, cs_=cs_, kb=kb, i=i, si=si: e.matmul(bank[OB[g]][0:64, cs_], lhsT=v_sb[:, kb, 0:64], rhs=p_sb[i][:, cs_],
                                                                           start=(si == 0), stop=(si == len(pr) - 1)),
                       reads=[b_v[kb], b_p[i]], writes=[b_bank[OB[g]]])
                for si, (kb, slot, i) in enumerate(pr):
                    op("pe", lambda e, cs_=cs_, i=i, si=si: e.matmul(bank[DB[g]][0:64, cs_], lhsT=ones_bf[:, 0:64], rhs=p_sb[i][:, cs_],
                                                                    start=(si == 0), stop=(si == len(pr) - 1)),
                       reads=[b_ones, b_p[i]], writes=[b_bank[DB[g]]])
            op("dve", lambda e: e.tensor_tensor(out=dn[:], in0=bank[DB[g]][0:64, :], in1=esk_t[hg][:], op=ALU.add),
               reads=[b_bank[DB[g]], b_eskt[hg]], writes=[b_dn])
            op("dve", lambda e: e.reciprocal(out=dn[:], in_=dn[:]), reads=[b_dn], writes=[b_dn])
            op("dve", lambda e: e.tensor_tensor(out=ofc[g][:], in0=bank[OB[g]][0:64, :], in1=dn[:], op=ALU.mult),
               reads=[b_bank[OB[g]], b_dn], writes=[b_ofc[g]])
            for hh in range(4):
                hd = hh * 2 + hg
                op("sp", lambda e, hh=hh, hd=hd: e.dma_start(out=oT[hd * 64:(hd + 1) * 64, nb * 128:(nb + 1) * 128],
                                                             in_=ofc[g][:, hh * 128:(hh + 1) * 128]),
                   reads=[b_ofc[g]], writes=[b_out[g]], dma_key=f"out{g}")

        for nb in range(nkb):
            for hg in range(2):
                c_block(nb, hg)

    op("sp", lambda e: None, reads=b_out)
    Sx.emit(nc, es)
    return Sx


BATCH, SEQ, DEPTH = 4, 4096, 4
_PROGS = {}


def _prog(key):
    if key not in _PROGS:
        if key == "tail":
            _PROGS[key] = tail_program(SEQ // 2)
        else:
            _PROGS[key] = attn_program(key, SEQ)
    return _PROGS[key]


def _consts(mode):
    p = np.arange(128)
    rc = np.zeros((128, 4), np.float32)
    rc[:, 0] = (10000.0 ** (-(2.0 * (p % 32)) / 64.0)).astype(np.float32)
    rc[:, 1] = np.where((p % 64) < 32, -1.0, 1.0)
    cm = np.zeros((128, 256), np.float32)
    if mode in ("A", "C"):
        for m in range(128):
            k = m + 32 if (m % 64) < 32 else m - 32
            cm[k, m] = 1.0
    else:
        jj, ss = np.meshgrid(p, p, indexing="ij")
        cm[:, 0:128] = np.where(jj >= ss, -1.0, 0.0)
    return rc, cm


def _col(v):
    return np.ascontiguousarray(np.asarray(v, np.float32).reshape(8, 128).T)


def kernel(x, positions, norm_gains, a_w_in, a_w_out, a_lambda, a_subln, b_w_in, b_w_out,
           c_w_in, c_w_out, c_sinks, ffn_w_gate, ffn_w_up, ffn_w_down):
    x = np.asarray(x, np.float32)
    positions = np.asarray(positions, np.int32)
    norm_gains = np.asarray(norm_gains, np.float32)
    cores = list(range(8))
    xT = [np.ascontiguousarray(x[b].T) for b in range(BATCH)]
    posb = [np.ascontiguousarray(np.broadcast_to(positions[b][None, :], (128, SEQ))) for b in range(BATCH)]
    half = SEQ // 2
    for i in range(DEPTH):
        kind, inst = i % 3, i // 3
        mode = "ABC"[kind]
        rc, cm = _consts(mode)
        g0 = _col(norm_gains[i, 0])
        in_maps = []
        for b in range(BATCH):
            for hh in range(2):
                extra = np.zeros((128, 264), np.float32)
                if mode == "A":
                    w = np.asarray(a_w_in[inst], np.float32)
                    wq, wk, wv = (w[:, o + hh * 512:o + (hh + 1) * 512] for o in (0, 1024, 2048))
                    lam_init = 0.8 - 0.6 * math.exp(-0.3 * i)
                    extra[:, 0:256] = np.asarray(a_lambda[inst], np.float32).reshape(1, 256)
                    extra[:, 256] = np.asarray(a_subln[inst], np.float32)
                    extra[:, 257] = lam_init
                    extra[:, 258] = 1.0 - lam_init
                    w_out = a_w_out[inst]
                elif mode == "B":
                    w = np.asarray(b_w_in[inst], np.float32)
                    wq, wk, wv = (w[:, o + hh * 512:o + (hh + 1) * 512] for o in (0, 1024, 2048))
                    w_out = b_w_out[inst]
                else:
                    w = np.asarray(c_w_in[inst], np.float32)
                    wq = w[:, hh * 512:(hh + 1) * 512]
                    wk = w[:, 1024 + hh * 64:1024 + (hh + 1) * 64]
                    wv = w[:, 1152 + hh * 64:1152 + (hh + 1) * 64]
                    extra[:, 0:8] = np.asarray(c_sinks[inst], np.float32)[hh * 8:(hh + 1) * 8].reshape(1, 8)
                    w_out = c_w_out[inst]
                in_maps.append({"xT": xT[b], "g0": g0, "wq": np.ascontiguousarray(wq), "wk": np.ascontiguousarray(wk),
                                "wv": np.ascontiguousarray(wv), "cmat": cm, "rc": rc, "pos": posb[b], "extra": extra})
        res = run_bass_kernel_spmd(_prog(mode), in_maps, core_ids=cores)
        oT = [np.concatenate([res.results[b * 2 + hh]["oT"] for hh in range(2)], axis=0) for b in range(BATCH)]
        gains = np.ascontiguousarray(norm_gains[i, 1:4].reshape(3, 8, 128).transpose(2, 0, 1).reshape(128, 24))
        w_out = np.ascontiguousarray(np.asarray(w_out, np.float32))
        wg = np.ascontiguousarray(np.asarray(ffn_w_gate[i], np.float32))
        wu = np.ascontiguousarray(np.asarray(ffn_w_up[i], np.float32))
        wd = np.ascontiguousarray(np.asarray(ffn_w_down[i], np.float32))
        in_maps = []
        for b in range(BATCH):
            for th in range(2):
                sl = slice(th * half, (th + 1) * half)
                in_maps.append({"xT": np.ascontiguousarray(xT[b][:, sl]), "oT": np.ascontiguousarray(oT[b][:, sl]),
                                "w_out": w_out, "w_g": wg, "w_u": wu, "w_d": wd, "gains": gains})
        res = run_bass_kernel_spmd(_prog("tail"), in_maps, core_ids=cores)
        xT = [np.concatenate([res.results[b * 2 + th]["yT"] for th in range(2)], axis=1) for b in range(BATCH)]
    out = np.stack([np.ascontiguousarray(xT[b].T) for b in range(BATCH)], axis=0)
    return out.astype(np.float32)
```

```python
from contextlib import ExitStack
import math
import numpy as np
import concourse.bass as bass
import concourse.mybir as mybir
from concourse.bass_utils import run_bass_kernel_spmd

F32 = mybir.dt.float32
BF16 = mybir.dt.bfloat16
I32 = mybir.dt.int32
ALU = mybir.AluOpType
AF = mybir.ActivationFunctionType
AX = mybir.AxisListType


class Buf:
    __slots__ = ("name", "last_w", "readers", "psum")

    def __init__(self, name="", psum=False):
        self.name = name
        self.psum = psum
        self.last_w = None
        self.readers = []


class _Op:
    __slots__ = ("eng", "fn", "deps", "cnt", "dma_key", "idx", "has_dep", "sem_inc", "stage")


class Sched:
    def __init__(self):
        self.ops = []
        self.pending = {}
        self.last = {}
        self.last_key = {}
        self.stage = 0

    def barrier(self, bo):
        for e in ("pe", "act", "dve", "pool", "sp"):
            self.pending[e] = bo.idx
        self.stage += 1

    def op(self, eng, fn, reads=(), writes=(), dma_key=None, sem_inc=16, after_all=False):
        o = _Op()
        o.sem_inc = sem_inc
        o.stage = self.stage // 2
        o.eng = eng
        o.fn = fn
        o.idx = len(self.ops)
        o.dma_key = dma_key
        o.has_dep = False
        o.cnt = 0
        deps = set()
        for b in reads:
            if b.last_w is not None:
                deps.add(b.last_w)
            if b.psum:
                deps.update(r for r in b.readers if self.ops[r].eng != eng)
        for b in writes:
            if b.last_w is not None:
                deps.add(b.last_w)
            deps.update(b.readers)
        for b in reads:
            b.readers.append(o.idx)
        for b in writes:
            b.last_w = o.idx
            b.readers = []
        if eng in self.pending:
            deps.add(self.pending.pop(eng))
        if after_all:
            deps.update(self.last.values())
            deps.update(self.last_key.values())
        self.last[eng] = o.idx
        if dma_key is not None:
            self.last_key[dma_key] = o.idx
        deps.discard(o.idx)
        o.deps = deps
        self.ops.append(o)
        return o

    def emit(self, nc, es):
        ops = self.ops
        engs = ["pe", "act", "dve", "pool", "sp"]
        for o in ops:
            for d in o.deps:
                p = ops[d]
                if p.dma_key is None and p.eng == "pe" and o.eng == "pe" and o.dma_key is None:
                    continue
                p.has_dep = True
        ecnt = {}
        kcnt = {}
        for o in ops:
            if o.dma_key is not None:
                kcnt[o.dma_key] = kcnt.get(o.dma_key, 0) + o.sem_inc
                o.cnt = kcnt[o.dma_key]
            elif o.has_dep:
                ek = (o.eng, o.stage)
                ecnt[ek] = ecnt.get(ek, 0) + 1
                o.cnt = ecnt[ek]
        sems = {}
        for (e, st) in ecnt:
            sems[("e", e, st)] = es.enter_context(nc.semaphore(f"sem_{e}_{st}"))
        for k in kcnt:
            sems[("k", k)] = es.enter_context(nc.semaphore("semk_" + str(k)))
        self.nsems = len(sems)
        block = es.enter_context(nc.Block())
        known = {e: {} for e in engs}

        def run(ename, eng):
            kn = known[ename]
            for o in ops:
                if o.eng != ename:
                    continue
                need = {}
                for d in o.deps:
                    p = ops[d]
                    if p.dma_key is not None:
                        key = ("k", p.dma_key)
                    else:
                        if p.eng == "pe" and ename == "pe" and o.dma_key is None:
                            continue
                        key = ("e", p.eng, p.stage)
                    if p.cnt > need.get(key, 0):
                        need[key] = p.cnt
                for key, v in need.items():
                    if kn.get(key, 0) >= v:
                        continue
                    eng.wait_ge(sems[key], v)
                    kn[key] = v
                ins = o.fn(eng)
                if ins is None:
                    continue
                if o.dma_key is not None:
                    if o.sem_inc == 16:
                        ins.then_inc(sems[("k", o.dma_key)], 16)
                    else:
                        ins.then_inc(sems[("k", o.dma_key)])
                elif o.has_dep:
                    ins.then_inc(sems[("e", ename, o.stage)], 1)

        @block.tensor
        def _(e):
            run("pe", e)

        @block.scalar
        def _(e):
            run("act", e)

        @block.vector
        def _(e):
            run("dve", e)

        @block.gpsimd
        def _(e):
            run("pool", e)

        @block.sync
        def _(e):
            run("sp", e)


class Arena:
    def __init__(self, nc, es, nbytes):
        self.t = es.enter_context(nc.sbuf_tensor("arena", [128, nbytes], mybir.dt.uint8))
        self.n = nbytes
        self.off = 0

    def reset(self):
        self.off = 0

    def alloc(self, shape, dt):
        esz = {F32: 4, I32: 4, BF16: 2}[dt]
        free = 1
        for d in shape[1:]:
            free *= d
        nb = free * esz
        off = (self.off + 63) // 64 * 64
        assert off + nb <= self.n, ("arena overflow", off, nb, self.n)
        self.off = off + nb
        ap = self.t[0:shape[0], off:off + nb].bitcast(dt)
        if len(shape) == 3:
            ap = ap.rearrange("p (a b) -> p a b", a=shape[1])
        return ap


_PIDS = {}


def PID(e):
    k = id(e)
    if k not in _PIDS:
        _PIDS[k] = e.partition_id()
    return _PIDS[k]

D = 1024
HID = 2816
NHC = HID // 128
EPS = 1e-6
NT = 512
PI = math.pi


def emit_tail(cx, T, x_src, o_src, w_out, w_g, w_u, w_d, gains, x_dst, hn_dst, b_in, b_outs):
    nc, S, bank, b_bank = cx.nc, cx.S, cx.bank, cx.b_bank
    cx.arena.reset()
    sb = lambda name, shape, dt: cx.arena.alloc(shape, dt)
    ntile = T // NT

    wo_bf = sb("wo_bf", [128, 8, D], BF16)
    g_sb = sb("g_sb", [128, 32], F32)
    ones_bf = sb("ones_bf", [128, 128], BF16)
    x_sb = [sb(f"x_sb{i}", [128, 8, NT], F32) for i in range(2)]
    o_bf = [sb(f"o_bf{i}", [128, 8, NT], BF16) for i in range(2)]
    m_sb = sb("m_sb", [128, 8, NT], F32)
    sq = [sb(f"sq{i}", [128, NT], BF16) for i in range(2)]
    lnv = sb("lnv", [128, NT], F32)
    rstd = sb("rstd", [128, NT], F32)
    tmp = [sb(f"tmp{i}", [128, NT], F32) for i in range(2)]
    hn = sb("hn", [128, 8, NT], BF16)
    h = sb("h", [128, NHC, NT], BF16)
    sl = [sb(f"sl{i}", [128, NT], F32) for i in range(2)]
    wg_b = [sb(f"wg_b{i}", [128, 8, 512], BF16) for i in range(2)]
    wu_b = [sb(f"wu_b{i}", [128, 8, 512], BF16) for i in range(2)]
    wd_b = [sb(f"wd_b{i}", [128, NHC, 512], BF16) for i in range(2)]
    pm = [bank[0], bank[1]]
    pss = bank[2]
    pg = [bank[3], bank[4]]
    pu = [bank[5], bank[6]]

    B = lambda n: Buf(n)
    b_wo, b_g, b_ones = B("wo"), B("g"), B("ones")
    b_x = [[B("x") for _ in range(8)] for _ in range(2)]
    b_o = [B("o0"), B("o1")]
    b_m = [B("m") for _ in range(8)]
    b_sq = [B("sq0"), B("sq1")]
    b_lnv, b_rstd = B("lnv"), B("rstd")
    b_tmp = [B("t0"), B("t1")]
    b_hn = [B("hn") for _ in range(8)]
    b_h = [B("h") for _ in range(NHC)]
    b_sl = [B("sl0"), B("sl1")]
    b_wg = [B("wg0"), B("wg1")]
    b_wu = [B("wu0"), B("wu1")]
    b_wd = [B("wd0"), B("wd1")]
    b_pm = [b_bank[0], b_bank[1]]
    b_pss = b_bank[2]
    b_pg = [b_bank[3], b_bank[4]]
    b_pu = [b_bank[5], b_bank[6]]
    b_y = [B("y0"), B("y1")]

    wov = w_out.rearrange("(c p) n -> p c n", p=128)
    wgv = w_g.rearrange("(c p) n -> p c n", p=128)
    wuv = w_u.rearrange("(c p) n -> p c n", p=128)
    wdv = w_d.rearrange("(c p) n -> p c n", p=128)

    S.op("sp", lambda e: e.dma_start(out=g_sb[:], in_=gains), writes=[b_g], dma_key="misc")
    S.op("dve", lambda e: e.memset(ones_bf[:], 1.0), writes=[b_ones])
    for half in range(2):
        S.op("pool", lambda e, half=half: e.dma_start(out=wo_bf[:, :, half * 512:(half + 1) * 512],
                                                      in_=wov[:, :, half * 512:(half + 1) * 512]),
             writes=[b_wo], dma_key="wo")

    groups = []
    c0 = 0
    while c0 < NHC:
        n = min(4, NHC - c0)
        groups.append((c0, n))
        c0 += n

    gcount = [0]

    def load_group(gi):
        c0, n = groups[gi]
        s = gcount[0] % 2
        gcount[0] += 1
        S.op("pool", lambda e: e.dma_start(out=wg_b[s][:, :, 0:n * 128], in_=wgv[:, :, c0 * 128:(c0 + n) * 128]),
             writes=[b_wg[s]], dma_key=f"wg{s}")
        S.op("pool", lambda e: e.dma_start(out=wu_b[s][:, :, 0:n * 128], in_=wuv[:, :, c0 * 128:(c0 + n) * 128]),
             writes=[b_wu[s]], dma_key=f"wu{s}")
        return s

    def load_wd(half):
        for (a, b) in ((0, 8), (8, 16), (16, NHC)):
            S.op("pool", lambda e, a=a, b=b: e.dma_start(out=wd_b[half][:, a:b, :],
                                                         in_=wdv[:, a:b, half * 512:(half + 1) * 512]),
                 writes=[b_wd[half]], dma_key=f"wd{half}")

    def load_tile(tt):
        s = tt % 2
        t0 = tt * NT
        S.op("sp", lambda e: e.dma_start(out=x_sb[s][:], in_=x_src(e, t0)),
             reads=b_in, writes=b_x[s], dma_key=f"x{s}")
        S.op("sp", lambda e: e.dma_start(out=o_bf[s][:], in_=o_src(e, t0)),
             reads=b_in, writes=[b_o[s]], dma_key=f"o{s}")

    def norm_stats_from(src_ap_fn, src_bufs_fn, oc):
        q = oc % 2
        S.op("act", lambda e: e.activation(out=sq[q][:], in_=src_ap_fn(oc), func=AF.Square),
             reads=src_bufs_fn(oc), writes=[b_sq[q]])
        S.op("pe", lambda e: e.matmul(pss[:], lhsT=ones_bf[:], rhs=sq[q][:], start=(oc == 0), stop=(oc == 7)),
             reads=[b_ones, b_sq[q]], writes=[b_pss])

    def make_rstd():
        S.op("act", lambda e: e.activation(out=lnv[:], in_=pss[:], func=AF.Ln, bias=EPS, scale=1.0 / D),
             reads=[b_pss], writes=[b_lnv])
        S.op("act", lambda e: e.activation(out=rstd[:], in_=lnv[:], func=AF.Exp, scale=-0.5),
             reads=[b_lnv], writes=[b_rstd])

    def residual_add(s, gj):
        for oc in range(8):
            q = oc % 2
            S.op("dve", lambda e, oc=oc, q=q: e.scalar_tensor_tensor(
                out=tmp[q][:], in0=m_sb[:, oc, :], scalar=g_sb[:, gj * 8 + oc:gj * 8 + oc + 1], in1=rstd[:],
                op0=ALU.mult, op1=ALU.mult), reads=[b_m[oc], b_g, b_rstd], writes=[b_tmp[q]])
            S.op("dve", lambda e, oc=oc, q=q: e.tensor_tensor(
                out=x_sb[s][:, oc, :], in0=x_sb[s][:, oc, :], in1=tmp[q][:], op=ALU.add),
                reads=[b_tmp[q], b_x[s][oc]], writes=[b_x[s][oc]])

    load_tile(0)

    def tile_body(tt):
        s = tt % 2
        if tt + 1 < ntile:
            load_tile(tt + 1)
        for oc in range(8):
            p = oc % 2
            for kc in range(8):
                S.op("pe", lambda e, oc=oc, kc=kc, p=p: e.matmul(
                    pm[p][:], lhsT=wo_bf[:, kc, oc * 128:(oc + 1) * 128], rhs=o_bf[s][:, kc, :],
                    start=(kc == 0), stop=(kc == 7)), reads=[b_wo, b_o[s]], writes=[b_pm[p]])
            S.op("dve", lambda e, oc=oc, p=p: e.tensor_copy(out=m_sb[:, oc, :], in_=pm[p][:]),
                 reads=[b_pm[p]], writes=[b_m[oc]])
            norm_stats_from(lambda oc_, p=p: pm[p][:], lambda oc_, p=p: [b_pm[p]], oc)
        make_rstd()
        residual_add(s, 0)
        for oc in range(8):
            norm_stats_from(lambda oc_: x_sb[s][:, oc_, :], lambda oc_: [b_x[s][oc_]], oc)
        make_rstd()
        for oc in range(8):
            S.op("dve", lambda e, oc=oc: e.scalar_tensor_tensor(
                out=hn[:, oc, :], in0=x_sb[s][:, oc, :], scalar=g_sb[:, 8 + oc:8 + oc + 1], in1=rstd[:],
                op0=ALU.mult, op1=ALU.mult), reads=[b_x[s][oc], b_g, b_rstd], writes=[b_hn[oc]])
        slot = load_group(0)
        for gi, (c0, n) in enumerate(groups):
            cur = slot
            if gi + 1 < len(groups):
                slot = load_group(gi + 1)
            elif True:
                load_wd(0)
                load_wd(1)
            for j in range(n):
                hc = c0 + j
                p = hc % 2
                for kc in range(8):
                    S.op("pe", lambda e, j=j, kc=kc, p=p, cur=cur: e.matmul(
                        pg[p][:], lhsT=wg_b[cur][:, kc, j * 128:(j + 1) * 128], rhs=hn[:, kc, :],
                        start=(kc == 0), stop=(kc == 7)), reads=[b_wg[cur], b_hn[kc]], writes=[b_pg[p]])
                for kc in range(8):
                    S.op("pe", lambda e, j=j, kc=kc, p=p, cur=cur: e.matmul(
                        pu[p][:], lhsT=wu_b[cur][:, kc, j * 128:(j + 1) * 128], rhs=hn[:, kc, :],
                        start=(kc == 0), stop=(kc == 7)), reads=[b_wu[cur], b_hn[kc]], writes=[b_pu[p]])
                S.op("act", lambda e, p=p: e.activation(out=sl[p][:], in_=pg[p][:], func=AF.Silu),
                     reads=[b_pg[p]], writes=[b_sl[p]])
                S.op("dve", lambda e, p=p, hc=hc: e.tensor_tensor(
                    out=h[:, hc, :], in0=sl[p][:], in1=pu[p][:], op=ALU.mult),
                    reads=[b_sl[p], b_pu[p]], writes=[b_h[hc]])
        for oc in range(8):
            p = oc % 2
            half, j = oc // 4, oc % 4
            for hc in range(NHC):
                S.op("pe", lambda e, hc=hc, p=p, half=half, j=j: e.matmul(
                    pm[p][:], lhsT=wd_b[half][:, hc, j * 128:(j + 1) * 128], rhs=h[:, hc, :],
                    start=(hc == 0), stop=(hc == NHC - 1)), reads=[b_wd[half], b_h[hc]], writes=[b_pm[p]])
            S.op("dve", lambda e, oc=oc, p=p: e.tensor_copy(out=m_sb[:, oc, :], in_=pm[p][:]),
                 reads=[b_pm[p]], writes=[b_m[oc]])
            norm_stats_from(lambda oc_, p=p: pm[p][:], lambda oc_, p=p: [b_pm[p]], oc)
        make_rstd()
        residual_add(s, 2)
        t0 = tt * NT
        S.op("sp", lambda e, s=s, t0=t0: e.dma_start(out=x_dst(e, t0), in_=x_sb[s][:]),
             reads=b_x[s], writes=[b_y[s]] + b_outs, dma_key=f"st{s}")
        if hn_dst is not None:
            for oc in range(8):
                norm_stats_from(lambda oc_: x_sb[s][:, oc_, :], lambda oc_: [b_x[s][oc_]], oc)
            make_rstd()
            for oc in range(8):
                S.op("dve", lambda e, oc=oc: e.scalar_tensor_tensor(
                    out=hn[:, oc, :], in0=x_sb[s][:, oc, :], scalar=g_sb[:, 24 + oc:24 + oc + 1], in1=rstd[:],
                    op0=ALU.mult, op1=ALU.mult), reads=[b_x[s][oc], b_g, b_rstd], writes=[b_hn[oc]])
            S.op("sp", lambda e, t0=t0: e.dma_start(out=hn_dst(e, t0), in_=hn[:]),
                 reads=b_hn, writes=[b_y[s]] + b_outs, dma_key=f"sh{s}")
    for tt in range(ntile):
        tile_body(tt)
    return b_y


def emit_attn(cx, mode, S, io, layer0, b_in, b_outs):
    nc, Sx, bank, b_bank = cx.nc, cx.S, cx.bank, cx.b_bank
    cx.arena.reset()
    sb = lambda name, shape, dt: cx.arena.alloc(shape, dt)
    op = Sx.op
    ntile = S // NT
    nkb = S // 128
    KW = 64 if mode == "C" else 512
    rope = mode in ("A", "C")
    oT = io["oT"]

    g_sb = sb("g_sb", [128, 8], F32)
    rc_sb = sb("rc_sb", [128, 4], F32)
    ex_sb = sb("ex_sb", [128, 264], F32)
    cm_bf = sb("cm_bf", [128, 256], BF16)
    ones_bf = sb("ones_bf", [128, 128], BF16)
    wq_bf = sb("wq_bf", [128, 8, 512], BF16)
    KWk = 128 if mode == "C" else 512
    wk_bf = sb("wk_bf", [128, 8, KWk], BF16)
    wv_bf = sb("wv_bf", [128, 8, KW], BF16)
    b_g, b_rc, b_ex, b_cm, b_ones, b_wq, b_wk, b_wv = [Buf(n) for n in "g rc ex cm ones wq wk wv".split()]
    op("sp", lambda e: e.dma_start(out=g_sb[:], in_=io["g0"]), writes=[b_g], dma_key="c0")
    op("sp", lambda e: e.dma_start(out=rc_sb[:], in_=io["rc"]), writes=[b_rc], dma_key="c1")
    op("sp", lambda e: e.dma_start(out=ex_sb[:], in_=io["extra"]), writes=[b_ex], dma_key="c2")
    op("pool", lambda e: e.dma_start(out=cm_bf[:], in_=io["cmat"]), writes=[b_cm], dma_key="c3")
    op("dve", lambda e: e.memset(ones_bf[:], 1.0), writes=[b_ones])
    wqv = io["wq"].rearrange("(c p) n -> p c n", p=128)
    wkv = io["wk"].rearrange("(c p) n -> p c n", p=128)
    wvv = io["wv"].rearrange("(c p) n -> p c n", p=128)
    op("pool", lambda e: e.dma_start(out=wq_bf[:], in_=wqv), writes=[b_wq], dma_key="w0")
    if mode == "C":
        op("pool", lambda e: e.dma_start(out=wk_bf[:, :, 0:64], in_=wkv), writes=[b_wk], dma_key="w1")
        op("pool", lambda e: e.dma_start(out=wk_bf[:, :, 64:128], in_=wkv), writes=[b_wk], dma_key="w1")
    else:
        op("pool", lambda e: e.dma_start(out=wk_bf[:], in_=wkv), writes=[b_wk], dma_key="w1")
    op("pool", lambda e: e.dma_start(out=wv_bf[:], in_=wvv), writes=[b_wv], dma_key="w2")

    if rope:
        Ctab = sb("Ctab", [128, NT], F32)
        Stab = sb("Stab", [128, NT], F32)
        pos_i = sb("pos_i", [128, NT], I32)
        ang = sb("ang", [128, NT], F32)
        tr = sb("tr", [128, NT], F32)
        tr2 = sb("tr2", [128, NT], F32)
        b_tr2 = Buf("tr2")
        b_C = [Buf("C")] * ntile
        b_S = [Buf("S")] * ntile
        b_pos, b_ang, b_tr = Buf("pos"), Buf("ang"), Buf("tr")
        negpi = sb("negpi", [128, 1], F32)
        b_negpi = Buf("negpi")
        op("dve", lambda e: e.memset(negpi[:], -PI), writes=[b_negpi])
    def rope_tables(tt):
        if True:
            sl_ = slice(tt * NT, (tt + 1) * NT)
            op("sp", lambda e, sl_=sl_: e.dma_start(out=pos_i[:], in_=io["pos"][:, sl_]), writes=[b_pos], dma_key="pos")
            op("dve", lambda e: e.tensor_copy(out=ang[:], in_=pos_i[:]), reads=[b_pos], writes=[b_ang])
            op("dve", lambda e: e.tensor_scalar(out=ang[:], in0=ang[:], scalar1=rc_sb[:, 0:1], scalar2=None, op0=ALU.mult),
               reads=[b_ang, b_rc], writes=[b_ang])
            for (tab, b_tab, shift, sgn) in ((Stab, b_S, 0.0, True), (Ctab, b_C, 0.5 * PI, False)):
                op("dve", lambda e, shift=shift: e.tensor_scalar(out=tr[:], in0=ang[:], scalar1=shift, scalar2=1.0 / (2 * PI),
                                                                 op0=ALU.add, op1=ALU.mult), reads=[b_ang], writes=[b_tr])
                op("dve", lambda e: e.tensor_copy(out=pos_i[:], in_=tr[:]), reads=[b_tr], writes=[b_pos])
                op("dve", lambda e: e.tensor_copy(out=tr[:], in_=pos_i[:]), reads=[b_pos], writes=[b_tr])
                op("dve", lambda e: e.tensor_scalar(out=tr[:], in0=tr[:], scalar1=-2 * PI, scalar2=None, op0=ALU.mult),
                   reads=[b_tr], writes=[b_tr])
                op("dve", lambda e, shift=shift: e.scalar_tensor_tensor(out=tr[:], in0=ang[:], scalar=shift, in1=tr[:],
                                                                        op0=ALU.add, op1=ALU.add), reads=[b_ang, b_tr], writes=[b_tr])
                op("dve", lambda e: e.tensor_scalar(out=tr2[:], in0=tr[:], scalar1=PI, scalar2=-2 * PI, op0=ALU.is_gt, op1=ALU.mult),
                   reads=[b_tr], writes=[b_tr2])
                op("dve", lambda e: e.tensor_tensor(out=tr[:], in0=tr[:], in1=tr2[:], op=ALU.add), reads=[b_tr, b_tr2], writes=[b_tr])
                op("act", lambda e, sl_=sl_, tab=tab: e.activation(out=tab[:], in_=tr[:], func=AF.Sin),
                   reads=[b_tr], writes=[b_tab[tt]])
                if sgn:
                    op("dve", lambda e, sl_=sl_, tab=tab: e.tensor_scalar(out=tab[:], in0=tab[:], scalar1=rc_sb[:, 1:2],
                                                                          scalar2=None, op0=ALU.mult),
                       reads=[b_tab[tt], b_rc], writes=[b_tab[tt]])

    nqc = 4
    nkc = 1 if mode == "C" else 4
    qT = sb("qT", [128, nqc, S], BF16)
    kT = sb("kT", [128, nkc, S], BF16)
    v_sb = sb("v_sb", [128, nkb, KW], BF16)
    b_q = [[Buf("q") for _ in range(ntile)] for _ in range(nqc)]
    b_k = [[Buf("k") for _ in range(ntile)] for _ in range(nkc)]
    b_v = [Buf("v") for _ in range(nkb)]
    x_sb = sb("x_sb", [128, 8, NT], F32)
    hn = sb("hn", [128, 8, NT], BF16)
    sq = [sb(f"sq{i}", [128, NT], BF16) for i in range(2)]
    lnv = sb("lnv", [128, NT], F32)
    rstd = sb("rstd", [128, NT], F32)
    qb = [sb(f"qb{i}", [128, NT], BF16) for i in range(2)]
    t1 = [sb(f"t1{i}", [128, NT], F32) for i in range(2)]
    t2 = [sb(f"t2{i}", [128, NT], F32) for i in range(2)]
    b_x = [Buf("x") for _ in range(8)]
    b_hn = [Buf("hn") for _ in range(8)]
    b_sq = [Buf("sq0"), Buf("sq1")]
    b_lnv, b_rstd = Buf("lnv"), Buf("rstd")
    b_qb = [Buf("qb0"), Buf("qb1")]
    b_t1 = [Buf("t10"), Buf("t11")]
    b_t2 = [Buf("t20"), Buf("t21")]
    PP = [0, 1]
    PSW = [2, 3]
    PSS = 4
    cnt = [0]

    def proj_feat(tt, w_bf, b_w, c, npart, dstT, b_dst, do_rope, scale):
        sl_ = slice(tt * NT, (tt + 1) * NT)
        i = cnt[0] % 2
        cnt[0] += 1
        pb, sw = PP[i], PSW[i]
        for kc in range(8):
            op("pe", lambda e, kc=kc: e.matmul(bank[pb][0:npart, :], lhsT=w_bf[:, kc, c * 128:c * 128 + npart], rhs=hn[:, kc, :],
                                               start=(kc == 0), stop=(kc == 7)), reads=[b_w, b_hn[kc]], writes=[b_bank[pb]])
        if not do_rope:
            if scale == 1.0:
                op("act", lambda e: e.copy(out=dstT[0:npart, c, sl_], in_=bank[pb][0:npart, :]), reads=[b_bank[pb]], writes=[b_dst])
            else:
                op("dve", lambda e: e.tensor_scalar(out=dstT[0:npart, c, sl_], in0=bank[pb][0:npart, :], scalar1=scale, scalar2=None,
                                                    op0=ALU.mult), reads=[b_bank[pb]], writes=[b_dst])
            return
        op("act", lambda e: e.copy(out=qb[i][0:npart, :], in_=bank[pb][0:npart, :]), reads=[b_bank[pb]], writes=[b_qb[i]])
        op("pe", lambda e: e.matmul(bank[sw][0:npart, :], lhsT=cm_bf[0:npart, 0:npart], rhs=qb[i][0:npart, :], start=True, stop=True),
           reads=[b_cm, b_qb[i]], writes=[b_bank[sw]])
        op("dve", lambda e: e.tensor_tensor(out=t1[i][0:npart, :], in0=bank[pb][0:npart, :], in1=Ctab[0:npart, :], op=ALU.mult),
           reads=[b_bank[pb], b_C[tt]], writes=[b_t1[i]])
        op("dve", lambda e: e.tensor_tensor(out=t2[i][0:npart, :], in0=bank[sw][0:npart, :], in1=Stab[0:npart, :], op=ALU.mult),
           reads=[b_bank[sw], b_S[tt]], writes=[b_t2[i]])
        op("pool", lambda e: e.tensor_tensor(out=dstT[0:npart, c, sl_], in0=t1[i][0:npart, :], in1=t2[i][0:npart, :], op=ALU.add),
           reads=[b_t1[i], b_t2[i]], writes=[b_dst])

    def proj_tile(tt):
        sl_ = slice(tt * NT, (tt + 1) * NT)
        if rope:
            rope_tables(tt)
        if not layer0:
            op("sp", lambda e: e.dma_start(out=hn[:], in_=io["hn_src"](e, tt)), reads=b_in, writes=b_hn, dma_key="x")
        else:
            op("sp", lambda e: e.dma_start(out=x_sb[:], in_=io["x_src"](e, tt)), reads=b_in, writes=b_x, dma_key="x")
        for kc in range(8 if layer0 else 0):
            q_ = kc % 2
            op("act", lambda e, kc=kc, q_=q_: e.activation(out=sq[q_][:], in_=x_sb[:, kc, :], func=AF.Square),
               reads=[b_x[kc]], writes=[b_sq[q_]])
            op("pe", lambda e, kc=kc, q_=q_: e.matmul(bank[PSS][:], lhsT=ones_bf[:], rhs=sq[q_][:], start=(kc == 0), stop=(kc == 7)),
               reads=[b_ones, b_sq[q_]], writes=[b_bank[PSS]])
        if layer0:
            op("act", lambda e: e.activation(out=lnv[:], in_=bank[PSS][:], func=AF.Ln, bias=EPS, scale=1.0 / D),
               reads=[b_bank[PSS]], writes=[b_lnv])
            op("act", lambda e: e.activation(out=rstd[:], in_=lnv[:], func=AF.Exp, scale=-0.5), reads=[b_lnv], writes=[b_rstd])
        for kc in range(8 if layer0 else 0):
            op("dve", lambda e, kc=kc: e.scalar_tensor_tensor(out=hn[:, kc, :], in0=x_sb[:, kc, :], scalar=g_sb[:, kc:kc + 1], in1=rstd[:],
                                                             op0=ALU.mult, op1=ALU.mult),
               reads=[b_x[kc], b_g, b_rstd], writes=[b_hn[kc]])
        for c in range(nqc):
            proj_feat(tt, wq_bf, b_wq, c, 128, qT, b_q[c][tt], rope, 1.0)
        for c in range(nkc):
            proj_feat(tt, wk_bf, b_wk, c, 128, kT, b_k[c][tt], rope, 0.125 if mode == "B" else 1.0)
        for sub in range(4):
            i = cnt[0] % 2
            cnt[0] += 1
            pb = PP[i]
            kbi = tt * 4 + sub
            for kc in range(8):
                op("pe", lambda e, kc=kc, sub=sub, pb=pb: e.matmul(bank[pb][:, 0:KW], lhsT=hn[:, kc, sub * 128:(sub + 1) * 128],
                                                                   rhs=wv_bf[:, kc, :], start=(kc == 0), stop=(kc == 7)),
                   reads=[b_wv, b_hn[kc]], writes=[b_bank[pb]])
            op("act", lambda e, kbi=kbi, pb=pb: e.copy(out=v_sb[:, kbi, :], in_=bank[pb][:, 0:KW]), reads=[b_bank[pb]], writes=[b_v[kbi]])

    for tt in range(ntile):
        proj_tile(tt)

    b_out = [Buf("out0"), Buf("out1")]
    if mode == "A":
        lp = ex_sb
        lt = sb("lt", [128, 128], F32)
        ls = sb("ls", [128, 8], F32)
        b_lt, b_ls = Buf("lt"), Buf("ls")
        op("dve", lambda e: e.tensor_tensor(out=lt[:, 0:64], in0=lp[:, 0:64], in1=lp[:, 64:128], op=ALU.mult), reads=[b_ex], writes=[b_lt])
        op("dve", lambda e: e.tensor_tensor(out=lt[:, 64:128], in0=lp[:, 128:192], in1=lp[:, 192:256], op=ALU.mult), reads=[b_ex, b_lt], writes=[b_lt])
        op("dve", lambda e: e.tensor_reduce(out=ls[:, 0:1], in_=lt[:, 0:64], axis=AX.X, op=ALU.add), reads=[b_lt], writes=[b_ls])
        op("dve", lambda e: e.tensor_reduce(out=ls[:, 1:2], in_=lt[:, 64:128], axis=AX.X, op=ALU.add), reads=[b_lt, b_ls], writes=[b_ls])
        op("act", lambda e: e.activation(out=ls[:, 2:4], in_=ls[:, 0:2], func=AF.Exp), reads=[b_ls], writes=[b_ls])
        op("dve", lambda e: e.tensor_tensor(out=ls[:, 4:5], in0=ls[:, 2:3], in1=ls[:, 3:4], op=ALU.subtract), reads=[b_ls], writes=[b_ls])
        op("dve", lambda e: e.tensor_scalar(out=ls[:, 5:6], in0=ls[:, 4:5], scalar1=lp[:, 257:258], scalar2=-1.0, op0=ALU.add, op1=ALU.mult),
           reads=[b_ls, b_ex], writes=[b_ls])
        op("dve", lambda e: e.tensor_tensor(out=ls[:, 6:7], in0=lp[:, 256:257], in1=lp[:, 258:259], op=ALU.mult), reads=[b_ls, b_ex], writes=[b_ls])
        pT = [sb(f"pT{i}", [128, NT], BF16) for i in range(4)]
        b_pT = [Buf(f"pT{i}") for i in range(4)]
        r0 = sb("r0", [128, NT], F32)
        ta = [sb(f"ta{i}", [128, NT], F32) for i in range(2)]
        ob = sb("ob", [128, NT], F32)
        osq = sb("osq", [128, NT], BF16)
        of = [sb(f"of{i}", [128, NT], BF16) for i in range(2)]
        b_r0, b_ob, b_osq = Buf("r0"), Buf("ob"), Buf("osq")
        b_ta = [Buf("ta0"), Buf("ta1")]
        b_of = [Buf("of0"), Buf("of1")]
        SC = [0, 1]
        NUM = [2, 3]
        DEN = [5, 6]
        SSB = 7
        pcnt = [0]
        ocnt = [0]
        for h in range(4):
            for qt in range(ntile):
                last = 4 * qt + 3
                for kb in range(last + 1):
                    j = kb - 4 * qt
                    col0 = j * 128 if j > 0 else 0
                    for c in range(2):
                        i = pcnt[0]
                        pcnt[0] += 1
                        sc, r = SC[i % 2], i % 4
                        ps_ = slice(c * 64, (c + 1) * 64)
                        op("pe", lambda e, kb=kb, col0=col0, sc=sc, ps_=ps_, h=h, qt=qt: e.matmul(
                            bank[sc][:, col0:NT], lhsT=kT[ps_, h, kb * 128:(kb + 1) * 128],
                            rhs=qT[ps_, h, qt * NT + col0:(qt + 1) * NT], start=True, stop=True),
                            reads=[b_k[h][kb // 4], b_q[h][qt]], writes=[b_bank[sc]])
                        op("act", lambda e, col0=col0, sc=sc, r=r: e.activation(out=pT[r][:, col0:NT], in_=bank[sc][:, col0:NT],
                                                                              func=AF.Exp, scale=0.125),
                           reads=[b_bank[sc]], writes=[b_pT[r]])
                        if j >= 0:
                            op("pool", lambda e, col0=col0, r=r: e.affine_select(
                                out=pT[r][:, col0:col0 + 128], in_=pT[r][:, col0:col0 + 128], pattern=[[1, 128]],
                                compare_op=ALU.is_ge, fill=0.0, base=0, channel_multiplier=-1),
                                reads=[b_pT[r]], writes=[b_pT[r]])
                        op("pe", lambda e, kb=kb, col0=col0, r=r, c=c, h=h, last=last: e.matmul(
                            bank[NUM[c]][:, col0:NT], lhsT=v_sb[:, kb, h * 128:(h + 1) * 128], rhs=pT[r][:, col0:NT],
                            start=(kb == 0), stop=(kb == last)), reads=[b_v[kb], b_pT[r]], writes=[b_bank[NUM[c]]])
                        op("pe", lambda e, kb=kb, col0=col0, r=r, c=c, last=last: e.matmul(
                            bank[DEN[c]][:, col0:NT], lhsT=ones_bf[:], rhs=pT[r][:, col0:NT],
                            start=(kb == 0), stop=(kb == last)), reads=[b_ones, b_pT[r]], writes=[b_bank[DEN[c]]])
                for c in range(2):
                    op("dve", lambda e, c=c: e.reciprocal(out=r0[:], in_=bank[DEN[c]][:]), reads=[b_bank[DEN[c]]], writes=[b_r0])
                    op("dve", lambda e, c=c: e.tensor_tensor(out=ta[c][:], in0=bank[NUM[c]][:], in1=r0[:], op=ALU.mult),
                       reads=[b_bank[NUM[c]], b_r0], writes=[b_ta[c]])
                op("dve", lambda e: e.scalar_tensor_tensor(out=ob[:], in0=ta[1][:], scalar=ls[:, 5:6], in1=ta[0][:], op0=ALU.mult, op1=ALU.add),
                   reads=[b_ta[0], b_ta[1], b_ls], writes=[b_ob])
                op("act", lambda e: e.activation(out=osq[:], in_=ob[:], func=AF.Square), reads=[b_ob], writes=[b_osq])
                op("pe", lambda e: e.matmul(bank[SSB][:], lhsT=ones_bf[:], rhs=osq[:], start=True, stop=True),
                   reads=[b_ones, b_osq], writes=[b_bank[SSB]])
                op("act", lambda e: e.activation(out=lnv[:], in_=bank[SSB][:], func=AF.Ln, bias=EPS, scale=1.0 / 128),
                   reads=[b_bank[SSB]], writes=[b_lnv])
                op("act", lambda e: e.activation(out=rstd[:], in_=lnv[:], func=AF.Exp, scale=-0.5), reads=[b_lnv], writes=[b_rstd])
                oi = ocnt[0] % 2
                ocnt[0] += 1
                op("dve", lambda e, oi=oi: e.scalar_tensor_tensor(out=of[oi][:], in0=ob[:], scalar=ls[:, 6:7], in1=rstd[:],
                                                                 op0=ALU.mult, op1=ALU.mult),
                   reads=[b_ob, b_ls, b_rstd], writes=[b_of[oi]])
                op("sp", lambda e, oi=oi, h=h, qt=qt: e.dma_start(out=oT[h * 128:(h + 1) * 128, qt * NT:(qt + 1) * NT], in_=of[oi][:]),
                   reads=[b_of[oi]], writes=[b_out[oi]] + b_outs, dma_key=f"out{oi}")
    if mode == "B":
        e_sb = [sb(f"e_sb{i}", [128, NT], F32) for i in range(3)]
        L_sb = [sb(f"L_sb{i}", [128, NT], BF16) for i in range(3)]
        a_sb = [sb(f"a_sb{i}", [128, NT], BF16) for i in range(3)]
        aarg = [sb(f"aarg{i}", [128, NT], F32) for i in range(2)]
        carry = sb("carry", [128, NT], F32)
        ofb = [sb(f"ofb{i}", [64, NT], BF16) for i in range(2)]
        b_e = [Buf("e") for _ in range(3)]
        b_L = [Buf("L") for _ in range(3)]
        b_a = [Buf("a") for _ in range(3)]
        b_aarg = [Buf("aarg") for _ in range(2)]
        b_carry = Buf("carry")
        b_ofb = [Buf("ofb0"), Buf("ofb1")]
        ARG = [0, 1, 2]
        CS = [3, 4]
        OPS = [5, 6]
        gq = [0]
        def b_group(h, qt):
            cch, ps_ = h // 2, slice((h % 2) * 64, (h % 2) * 64 + 64)
            blocks = list(range(4 * qt + 3, -1, -1))
            n = len(blocks)
            ob_ = OPS[gq[0] % 2]
            oi = gq[0] % 2
            gq[0] += 1

            def diag_mask(i, tile_, b_t):
                j = blocks[i] - 4 * qt
                if j >= 0:
                    op("pool", lambda e, j=j: e.affine_select(out=tile_[:], in_=tile_[:], pattern=[[1, NT]], compare_op=ALU.is_gt,
                                                              fill=0.0, base=-j * 128, channel_multiplier=-1),
                       reads=[b_t], writes=[b_t])

            def Z(i):
                kb = blocks[i]
                ab = ARG[i % 3]
                op("pe", lambda e: e.matmul(bank[ab][:], lhsT=kT[ps_, cch, kb * 128:(kb + 1) * 128], rhs=qT[ps_, cch, qt * NT:(qt + 1) * NT],
                                            start=True, stop=True), reads=[b_k[cch][kb // 4], b_q[cch][qt]], writes=[b_bank[ab]])

            def EL(i):
                ab, r = ARG[i % 3], i % 3
                op("act", lambda e: e.activation(out=e_sb[r][:], in_=bank[ab][:], func=AF.Exp), reads=[b_bank[ab]], writes=[b_e[r]])
                op("act", lambda e: e.activation(out=L_sb[r][:], in_=e_sb[r][:], func=AF.Ln, bias=1.0), reads=[b_e[r]], writes=[b_L[r]])
                diag_mask(i, L_sb[r], b_L[r])

            def CUM(i):
                ab, r, cb = ARG[i % 3], i % 3, CS[i % 2]
                op("pe", lambda e: e.matmul(bank[ab][:], lhsT=cm_bf[:, 0:128], rhs=L_sb[r][:], start=False, stop=True, skip_group_check=True),
                   reads=[b_cm, b_L[r]], writes=[b_bank[ab]])
                op("pe", lambda e: e.matmul(bank[cb][:], lhsT=ones_bf[:], rhs=L_sb[r][:], start=True, stop=True),
                   reads=[b_ones, b_L[r]], writes=[b_bank[cb]])

            def SUB(i):
                ab, cb, q_ = ARG[i % 3], CS[i % 2], i % 2
                if i == 0:
                    op("dve", lambda e: e.tensor_copy(out=aarg[q_][:], in_=bank[ab][:]), reads=[b_bank[ab]], writes=[b_aarg[q_]])
                    op("dve", lambda e: e.tensor_copy(out=carry[:], in_=bank[cb][:]), reads=[b_bank[cb]], writes=[b_carry])
                else:
                    op("dve", lambda e: e.tensor_tensor(out=aarg[q_][:], in0=bank[ab][:], in1=carry[:], op=ALU.subtract),
                       reads=[b_bank[ab], b_carry], writes=[b_aarg[q_]])
                    op("dve", lambda e: e.tensor_tensor(out=carry[:], in0=bank[cb][:], in1=carry[:], op=ALU.add),
                       reads=[b_bank[cb], b_carry], writes=[b_carry])

            def AA(i):
                r, q_ = i % 3, i % 2
                op("act", lambda e: e.activation(out=a_sb[r][:], in_=aarg[q_][:], func=AF.Exp), reads=[b_aarg[q_]], writes=[b_a[r]])
                diag_mask(i, a_sb[r], b_a[r])

            def PV(i):
                kb, r = blocks[i], i % 3
                op("pe", lambda e: e.matmul(bank[ob_][0:64, :], lhsT=v_sb[:, kb, h * 64:(h + 1) * 64], rhs=a_sb[r][:],
                                            start=(i == 0), stop=(i == n - 1)), reads=[b_v[kb], b_a[r]], writes=[b_bank[ob_]])

            Z(0)
            EL(0)
            if n > 1:
                Z(1)
            for i in range(n):
                CUM(i)
                if i + 1 < n:
                    EL(i + 1)
                if i + 2 < n:
                    Z(i + 2)
                SUB(i)
                AA(i)
                PV(i)
            op("dve", lambda e: e.tensor_copy(out=ofb[oi][:], in_=bank[ob_][0:64, :]), reads=[b_bank[ob_]], writes=[b_ofb[oi]])
            op("sp", lambda e: e.dma_start(out=oT[h * 64:(h + 1) * 64, qt * NT:(qt + 1) * NT], in_=ofb[oi][:]),
               reads=[b_ofb[oi]], writes=[b_out[oi]] + b_outs, dma_key=f"out{oi}")

        for h in range(8):
            for qt in range(ntile):
                b_group(h, qt)

    if mode == "C":
        esk = sb("esk", [128, 8], F32)
        zer = sb("zer", [64, 128], F32)
        esk_t = [sb(f"esk_t{i}", [64, NT], F32) for i in range(2)]
        p_sb = [sb(f"p_sb{i}", [128, NT], BF16) for i in range(4)]
        dn = sb("dn", [64, NT], F32)
        ofc = [sb(f"ofc{i}", [64, NT], BF16) for i in range(2)]
        b_esk, b_zer, b_dn = Buf("esk"), Buf("zer"), Buf("dn")
        b_eskt = [Buf("eskt0"), Buf("eskt1")]
        b_p = [Buf(f"p{i}") for i in range(4)]
        b_ofc = [Buf("ofc0"), Buf("ofc1")]
        op("act", lambda e: e.activation(out=esk[:], in_=ex_sb[:, 0:8], func=AF.Exp), reads=[b_ex], writes=[b_esk])
        op("dve", lambda e: e.memset(zer[:], 0.0), writes=[b_zer])
        for hd in range(8):
            op("dve", lambda e, hd=hd: e.tensor_scalar(out=esk_t[hd % 2][:, (hd // 2) * 128:(hd // 2 + 1) * 128], in0=zer[:],
                                                       scalar1=esk[0:64, hd:hd + 1], scalar2=None, op0=ALU.add),
               reads=[b_zer, b_esk, b_eskt[hd % 2]], writes=[b_eskt[hd % 2]])
        oview = oT.rearrange("(h d) s -> d h s", d=64)
        SCB = [0, 1, 2, 3]
        OB = [4, 5]
        DB = [6, 7]
        pc = [0]
        gq = [0]

        def c_block(nb, hg):
            slots = ([(nb - 1, 0)] if nb > 0 else []) + [(nb, 1)]
            pr = []
            for (kb, slot) in slots:
                i = pc[0] % 4
                pc[0] += 1
                pr.append((kb, slot, i))
                for hh in range(4):
                    hd = hh * 2 + hg
                    ps_ = slice((hd % 2) * 64, (hd % 2) * 64 + 64)
                    op("pe", lambda e, hh=hh, hd=hd, ps_=ps_, kb=kb, i=i: e.matmul(
                        bank[SCB[i]][:, hh * 128:(hh + 1) * 128], lhsT=kT[ps_, 0, kb * 128:(kb + 1) * 128],
                        rhs=qT[ps_, hd // 2, nb * 128:(nb + 1) * 128], start=True, stop=True),
                        reads=[b_k[0][kb // 4], b_q[hd // 2][nb // 4]], writes=[b_bank[SCB[i]]])
                op("act", lambda e, i=i: e.activation(out=p_sb[i][:], in_=bank[SCB[i]][:], func=AF.Exp, scale=0.125),
                   reads=[b_bank[SCB[i]]], writes=[b_p[i]])
                for hh in range(4):
                    cs_ = slice(hh * 128, (hh + 1) * 128)
                    if slot == 1:
                        op("pool", lambda e, i=i, cs_=cs_: e.affine_select(out=p_sb[i][:, cs_], in_=p_sb[i][:, cs_], pattern=[[1, 128]],
                                                                          compare_op=ALU.is_ge, fill=0.0, base=0, channel_multiplier=-1),
                           reads=[b_p[i]], writes=[b_p[i]])
                    else:
                        op("pool", lambda e, i=i, cs_=cs_: e.affine_select(out=p_sb[i][:, cs_], in_=p_sb[i][:, cs_], pattern=[[-1, 128]],
                                                                          compare_op=ALU.is_gt, fill=0.0, base=0, channel_multiplier=1),
                           reads=[b_p[i]], writes=[b_p[i]])
            g = gq[0] % 2
            gq[0] += 1
            for hh in range(4):
                cs_ = slice(hh * 128, (hh + 1) * 128)
                for si, (kb, slot, i) in enumerate(pr):
                    op("pe", lambda e, cs_=cs_, kb=kb, i=i, si=si: e.matmul(bank[OB[g]][0:64, cs_], lhsT=v_sb[:, kb, 0:64], rhs=p_sb[i][:, cs_],
                                                                           start=(si == 0), stop=(si == len(pr) - 1)),
                       reads=[b_v[kb], b_p[i]], writes=[b_bank[OB[g]]])
                for si, (kb, slot, i) in enumerate(pr):
                    op("pe", lambda e, cs_=cs_, i=i, si=si: e.matmul(bank[DB[g]][0:64, cs_], lhsT=ones_bf[:, 0:64], rhs=p_sb[i][:, cs_],
                                                                    start=(si == 0), stop=(si == len(pr) - 1)),
                       reads=[b_ones, b_p[i]], writes=[b_bank[DB[g]]])
            op("dve", lambda e: e.tensor_tensor(out=dn[:], in0=bank[DB[g]][0:64, :], in1=esk_t[hg][:], op=ALU.add),
               reads=[b_bank[DB[g]], b_eskt[hg]], writes=[b_dn])
            op("dve", lambda e: e.reciprocal(out=dn[:], in_=dn[:]), reads=[b_dn], writes=[b_dn])
            op("dve", lambda e: e.tensor_tensor(out=ofc[g][:], in0=bank[OB[g]][0:64, :], in1=dn[:], op=ALU.mult),
               reads=[b_bank[OB[g]], b_dn], writes=[b_ofc[g]])
            for hh in range(4):
                hd = hh * 2 + hg
                op("sp", lambda e, hh=hh, hd=hd: e.dma_start(out=oT[hd * 64:(hd + 1) * 64, nb * 128:(nb + 1) * 128],
                                                             in_=ofc[g][:, hh * 128:(hh + 1) * 128]),
                   reads=[b_ofc[g]], writes=[b_out[g]] + b_outs, dma_key=f"out{g}")

        for nb in range(nkb):
            for hg in range(2):
                c_block(nb, hg)

    return b_out


class _Cx:
    pass


def fused_program(SEQ, DEPTH):
    nc = bass.Bass("TRN2", target_bir_lowering=False)
    HALF = SEQ // 2
    tph = HALF // NT
    dr = lambda n, sh, dt=F32, kind="ExternalInput": nc.dram_tensor(n, sh, dt, kind=kind).ap()
    xT = dr("xT", [D, SEQ])
    pos = dr("pos", [128, SEQ], I32)
    rc = dr("rc", [128, 4])
    cmP = dr("cmP", [128, 256])
    cmT = dr("cmT", [128, 256])
    g0 = dr("g0", [128, 8])
    L = []
    for i in range(DEPTH):
        KW = 64 if i % 3 == 2 else 512
        L.append(dict(wq=dr(f"wq{i}", [D, 512]), wk=dr(f"wk{i}", [D, KW]), wv=dr(f"wv{i}", [D, KW]),
                      extra=dr(f"extra{i}", [128, 264]), w_out=dr(f"w_out{i}", [D, D]), w_g=dr(f"w_g{i}", [D, HID]),
                      w_u=dr(f"w_u{i}", [D, HID]), w_d=dr(f"w_d{i}", [HID, D]), gains=dr(f"gains{i}", [128, 32])))
    yT = dr("yT", [D, HALF], F32, kind="ExternalOutput")
    o_src = [nc.dram_tensor(f"o_src{i}", [512, SEQ], BF16) for i in range(DEPTH)]
    og = [nc.dram_tensor(f"og{i}", [8 * 512, SEQ], BF16) for i in range(DEPTH)]
    hn_src = [nc.dram_tensor(f"hn_src{i}", [D, HALF], BF16) for i in range(DEPTH - 1)]
    hng = [nc.dram_tensor(f"hng{i}", [8 * D, HALF], BF16) for i in range(DEPTH - 1)]
    x_res = nc.dram_tensor("x_res", [D, HALF], F32)
    fm = lambda ap_: ap_.rearrange("(c p) t -> p c t", p=128)
    with ExitStack() as es:
        cx = _Cx()
        cx.nc = nc
        cx.S = Sched()
        cx.bank = [es.enter_context(nc.psum_tensor(f"bank{i}", [128, NT], F32)) for i in range(8)]
        cx.b_bank = [Buf(f"bank{i}", psum=True) for i in range(8)]
        cx.arena = Arena(nc, es, 206 * 1024)
        S = cx.S
        b_xres, b_y = Buf("xres"), Buf("y")
        b_hng = None
        for i in range(DEPTH):
            mode = "ABC"[i % 3]
            last = (i == DEPTH - 1)
            io = dict(g0=g0, wq=L[i]["wq"], wk=L[i]["wk"], wv=L[i]["wv"], cmat=(cmT if mode == "B" else cmP), rc=rc, pos=pos,
                      extra=L[i]["extra"], oT=o_src[i].ap())
            io["x_src"] = lambda e, tt: fm(xT)[:, :, tt * NT:(tt + 1) * NT]
            if i > 0:
                hsrc = hng[i - 1]

                def hn_src_fn(e, tt, hsrc=hsrc):
                    pid = PID(e)
                    base = (pid - pid % 2) * D + (tt // tph) * D
                    return fm(hsrc.ap()[bass.ds(base, D), (tt % tph) * NT:(tt % tph + 1) * NT])
                io["hn_src"] = hn_src_fn
            b_osrc, b_og = Buf("osrc"), Buf("og")
            emit_attn(cx, mode, SEQ, io, i == 0, [b_hng] if i > 0 else [], [b_osrc])
            cc = S.op("pool", lambda e, i=i: e.collective_compute("AllGather", ALU.bypass, replica_groups=[list(range(8))],
                                                                 ins=[o_src[i].ap().opt()], outs=[og[i].ap().opt()]),
                      reads=[b_osrc], writes=[b_og], dma_key=f"cc{2 * i}", sem_inc=1, after_all=True)
            S.barrier(cc)

            def o_src_fn(e, t0, i=i):
                pid = PID(e)
                par = pid % 2
                return fm(og[i].ap()[bass.ds((pid - par) * 512, D), bass.ds(par * HALF + t0, NT)])
            if i == 0:
                def x_src_fn(e, t0):
                    par = PID(e) % 2
                    return fm(xT[:, bass.ds(par * HALF + t0, NT)])
            else:
                def x_src_fn(e, t0):
                    return fm(x_res.ap())[:, :, t0:t0 + NT]
            if last:
                x_dst_fn = lambda e, t0: fm(yT)[:, :, t0:t0 + NT]
                hn_dst_fn = None
                b_outs = [b_y]
            else:
                x_dst_fn = lambda e, t0: fm(x_res.ap())[:, :, t0:t0 + NT]
                hn_dst_fn = lambda e, t0, i=i: fm(hn_src[i].ap())[:, :, t0:t0 + NT]
                b_hnsrc = Buf("hnsrc")
                b_outs = [b_xres, b_hnsrc]
            emit_tail(cx, HALF, x_src_fn, o_src_fn, L[i]["w_out"], L[i]["w_g"], L[i]["w_u"], L[i]["w_d"], L[i]["gains"],
                      x_dst_fn, hn_dst_fn, [b_og, b_xres], b_outs)
            if not last:
                b_hng = Buf("hng")
                cc = S.op("pool", lambda e, i=i: e.collective_compute("AllGather", ALU.bypass, replica_groups=[list(range(8))],
                                                                     ins=[hn_src[i].ap().opt()], outs=[hng[i].ap().opt()]),
                          reads=[b_hnsrc, b_xres], writes=[b_hng], dma_key=f"cc{2 * i + 1}", sem_inc=1, after_all=True)
                S.barrier(cc)
        S.op("sp", lambda e: None, reads=[b_y], after_all=True)
        S.emit(nc, es)
    return nc


_PROG = {}


def _consts(mode):
    p = np.arange(128)
    rc = np.zeros((128, 4), np.float32)
    rc[:, 0] = (10000.0 ** (-(2.0 * (p % 32)) / 64.0)).astype(np.float32)
    rc[:, 1] = np.where((p % 64) < 32, -1.0, 1.0)
    cm = np.zeros((128, 256), np.float32)
    if mode in ("A", "C"):
        for m in range(128):
            k = m + 32 if (m % 64) < 32 else m - 32
            cm[k, m] = 1.0
    else:
        jj, ss = np.meshgrid(p, p, indexing="ij")
        cm[:, 0:128] = np.where(jj >= ss, -1.0, 0.0)
    return rc, cm


def _col(v):
    return np.ascontiguousarray(np.asarray(v, np.float32).reshape(8, 128).T)


def kernel(x, positions, norm_gains, a_w_in, a_w_out, a_lambda, a_subln, b_w_in, b_w_out,
           c_w_in, c_w_out, c_sinks, ffn_w_gate, ffn_w_up, ffn_w_down, _depth=None):
    x = np.asarray(x, np.float32)
    BATCH, SEQ, _ = x.shape
    DEPTH = _depth or int(np.asarray(norm_gains).shape[0])
    positions = np.asarray(positions, np.int32)
    norm_gains = np.asarray(norm_gains, np.float32)
    key = (SEQ, DEPTH)
    if key not in _PROG:
        _PROG[key] = fused_program(SEQ, DEPTH)
    rc, cmP = _consts("A")
    _, cmT = _consts("B")
    c32 = lambda a: np.ascontiguousarray(np.asarray(a, np.float32))
    shared = {"rc": rc, "cmP": cmP, "cmT": cmT, "g0": _col(norm_gains[0, 0])}
    for i in range(DEPTH):
        gl = np.ones((4, D), np.float32)
        gl[0:3] = norm_gains[i, 1:4]
        if i + 1 < DEPTH:
            gl[3] = norm_gains[i + 1, 0]
        shared[f"gains{i}"] = np.ascontiguousarray(gl.reshape(4, 8, 128).transpose(2, 0, 1).reshape(128, 32))
        inst = i // 3
        shared[f"w_out{i}"] = c32((a_w_out, b_w_out, c_w_out)[i % 3][inst])
        shared[f"w_g{i}"] = c32(ffn_w_gate[i])
        shared[f"w_u{i}"] = c32(ffn_w_up[i])
        shared[f"w_d{i}"] = c32(ffn_w_down[i])
    in_maps = []
    for b in range(BATCH):
        xTb = np.ascontiguousarray(x[b].T)
        posb = np.ascontiguousarray(np.broadcast_to(positions[b][None, :], (128, SEQ)))
        for hh in range(2):
            m = dict(shared)
            m["xT"] = xTb
            m["pos"] = posb
            for i in range(DEPTH):
                mode, inst = "ABC"[i % 3], i // 3
                extra = np.zeros((128, 264), np.float32)
                if mode == "A":
                    w = np.asarray(a_w_in[inst], np.float32)
                    wq, wk, wv = (w[:, o + hh * 512:o + (hh + 1) * 512] for o in (0, 1024, 2048))
                    lam_init = 0.8 - 0.6 * math.exp(-0.3 * i)
                    extra[:, 0:256] = np.asarray(a_lambda[inst], np.float32).reshape(1, 256)
                    extra[:, 256] = np.asarray(a_subln[inst], np.float32)
                    extra[:, 257] = lam_init
                    extra[:, 258] = 1.0 - lam_init
                elif mode == "B":
                    w = np.asarray(b_w_in[inst], np.float32)
                    wq, wk, wv = (w[:, o + hh * 512:o + (hh + 1) * 512] for o in (0, 1024, 2048))
                else:
                    w = np.asarray(c_w_in[inst], np.float32)
                    wq = w[:, hh * 512:(hh + 1) * 512]
                    wk = w[:, 1024 + hh * 64:1024 + (hh + 1) * 64]
                    wv = w[:, 1152 + hh * 64:1152 + (hh + 1) * 64]
                    extra[:, 0:8] = np.asarray(c_sinks[inst], np.float32)[hh * 8:(hh + 1) * 8].reshape(1, 8)
                m[f"wq{i}"], m[f"wk{i}"], m[f"wv{i}"], m[f"extra{i}"] = c32(wq), c32(wk), c32(wv), extra
            in_maps.append(m)
    res = run_bass_kernel_spmd(_PROG[key], in_maps, core_ids=list(range(2 * BATCH)))
    out = np.empty((BATCH, SEQ, D), np.float32)
    half = SEQ // 2
    for b in range(BATCH):
        for r in range(2):
            out[b, r * half:(r + 1) * half, :] = res.results[2 * b + r]["yT"].T
    return out
```

```python
from contextlib import ExitStack
import math
import numpy as np
import concourse.bass as bass
import concourse.mybir as mybir
from concourse.bass_utils import run_bass_kernel_spmd

F32 = mybir.dt.float32
BF16 = mybir.dt.bfloat16
I32 = mybir.dt.int32
ALU = mybir.AluOpType
AF = mybir.ActivationFunctionType
AX = mybir.AxisListType


class Buf:
    __slots__ = ("name", "last_w", "readers", "psum")

    def __init__(self, name="", psum=False):
        self.name = name
        self.psum = psum
        self.last_w = None
        self.readers = []


class _Op:
    __slots__ = ("eng", "fn", "deps", "cnt", "dma_key", "idx", "has_dep", "sem_inc", "stage")


class Sched:
    def __init__(self):
        self.ops = []
        self.pending = {}
        self.last = {}
        self.last_key = {}
        self.stage = 0

    def barrier(self, bo):
        for e in ("pe", "act", "dve", "pool", "sp"):
            self.pending[e] = bo.idx
        self.stage += 1

    def op(self, eng, fn, reads=(), writes=(), dma_key=None, sem_inc=16, after_all=False):
        o = _Op()
        o.sem_inc = sem_inc
        o.stage = self.stage // 2
        o.eng = eng
        o.fn = fn
        o.idx = len(self.ops)
        o.dma_key = dma_key
        o.has_dep = False
        o.cnt = 0
        deps = set()
        for b in reads:
            if b.last_w is not None:
                deps.add(b.last_w)
            if b.psum:
                deps.update(r for r in b.readers if self.ops[r].eng != eng)
        for b in writes:
            if b.last_w is not None:
                deps.add(b.last_w)
            deps.update(b.readers)
        for b in reads:
            b.readers.append(o.idx)
        for b in writes:
            b.last_w = o.idx
            b.readers = []
        if eng in self.pending:
            deps.add(self.pending.pop(eng))
        if after_all:
            deps.update(self.last.values())
            deps.update(self.last_key.values())
        self.last[eng] = o.idx
        if dma_key is not None:
            self.last_key[dma_key] = o.idx
        deps.discard(o.idx)
        o.deps = deps
        self.ops.append(o)
        return o

    def emit(self, nc, es):
        ops = self.ops
        engs = ["pe", "act", "dve", "pool", "sp"]
        for o in ops:
            for d in o.deps:
                p = ops[d]
                if p.dma_key is None and p.eng == "pe" and o.eng == "pe" and o.dma_key is None:
                    continue
                p.has_dep = True
        ecnt = {}
        kcnt = {}
        for o in ops:
            if o.dma_key is not None:
                kcnt[o.dma_key] = kcnt.get(o.dma_key, 0) + o.sem_inc
                o.cnt = kcnt[o.dma_key]
            elif o.has_dep:
                ek = (o.eng, o.stage)
                ecnt[ek] = ecnt.get(ek, 0) + 1
                o.cnt = ecnt[ek]
        sems = {}
        for (e, st) in ecnt:
            sems[("e", e, st)] = es.enter_context(nc.semaphore(f"sem_{e}_{st}"))
        for k in kcnt:
            sems[("k", k)] = es.enter_context(nc.semaphore("semk_" + str(k)))
        self.nsems = len(sems)
        block = es.enter_context(nc.Block())
        known = {e: {} for e in engs}

        def run(ename, eng):
            kn = known[ename]
            for o in ops:
                if o.eng != ename:
                    continue
                need = {}
                for d in o.deps:
                    p = ops[d]
                    if p.dma_key is not None:
                        key = ("k", p.dma_key)
                    else:
                        if p.eng == "pe" and ename == "pe" and o.dma_key is None:
                            continue
                        key = ("e", p.eng, p.stage)
                    if p.cnt > need.get(key, 0):
                        need[key] = p.cnt
                for key, v in need.items():
                    if kn.get(key, 0) >= v:
                        continue
                    eng.wait_ge(sems[key], v)
                    kn[key] = v
                ins = o.fn(eng)
                if ins is None:
                    continue
                if o.dma_key is not None:
                    if o.sem_inc == 16:
                        ins.then_inc(sems[("k", o.dma_key)], 16)
                    else:
                        ins.then_inc(sems[("k", o.dma_key)])
                elif o.has_dep:
                    ins.then_inc(sems[("e", ename, o.stage)], 1)

        @block.tensor
        def _(e):
            run("pe", e)

        @block.scalar
        def _(e):
            run("act", e)

        @block.vector
        def _(e):
            run("dve", e)

        @block.gpsimd
        def _(e):
            run("pool", e)

        @block.sync
        def _(e):
            run("sp", e)


class Arena:
    def __init__(self, nc, es, nbytes):
        self.t = es.enter_context(nc.sbuf_tensor("arena", [128, nbytes], mybir.dt.uint8))
        self.n = nbytes
        self.off = 0

    def reset(self):
        self.off = 0

    def alloc(self, shape, dt):
        esz = {F32: 4, I32: 4, BF16: 2}[dt]
        free = 1
        for d in shape[1:]:
            free *= d
        nb = free * esz
        off = (self.off + 63) // 64 * 64
        assert off + nb <= self.n, ("arena overflow", off, nb, self.n)
        self.off = off + nb
        ap = self.t[0:shape[0], off:off + nb].bitcast(dt)
        if len(shape) == 3:
            ap = ap.rearrange("p (a b) -> p a b", a=shape[1])
        return ap


_PIDS = {}


def PID(e):
    k = id(e)
    if k not in _PIDS:
        _PIDS[k] = e.partition_id()
    return _PIDS[k]

D = 1024
HID = 2816
NHC = HID // 128
EPS = 1e-6
NT = 512
PI = math.pi


def emit_tail(cx, T, x_src, o_src, w_out, w_g, w_u, w_d, gains, x_dst, hn_dst, b_in, b_outs):
    nc, S, bank, b_bank = cx.nc, cx.S, cx.bank, cx.b_bank
    cx.arena.reset()
    sb = lambda name, shape, dt: cx.arena.alloc(shape, dt)
    ntile = T // NT

    wo_bf = sb("wo_bf", [128, 8, D], BF16)
    g_sb = sb("g_sb", [128, 32], F32)
    ones_bf = sb("ones_bf", [128, 128], BF16)
    x_sb = [sb(f"x_sb{i}", [128, 8, NT], F32) for i in range(2)]
    o_bf = [sb(f"o_bf{i}", [128, 8, NT], BF16) for i in range(2)]
    m_sb = sb("m_sb", [128, 8, NT], F32)
    sq = [sb(f"sq{i}", [128, NT], BF16) for i in range(2)]
    lnv = sb("lnv", [128, NT], F32)
    rstd = sb("rstd", [128, NT], F32)
    tmp = [sb(f"tmp{i}", [128, NT], F32) for i in range(2)]
    hn = sb("hn", [128, 8, NT], BF16)
    h = sb("h", [128, NHC, NT], BF16)
    sl = [sb(f"sl{i}", [128, NT], F32) for i in range(2)]
    wg_b = [sb(f"wg_b{i}", [128, 8, 512], BF16) for i in range(2)]
    wu_b = [sb(f"wu_b{i}", [128, 8, 512], BF16) for i in range(2)]
    wd_b = [sb(f"wd_b{i}", [128, NHC, 512], BF16) for i in range(2)]
    pm = [bank[0], bank[1]]
    pss = bank[2]
    pg = [bank[3], bank[4]]
    pu = [bank[5], bank[6]]

    B = lambda n: Buf(n)
    b_wo, b_g, b_ones = B("wo"), B("g"), B("ones")
    b_x = [[B("x") for _ in range(8)] for _ in range(2)]
    b_o = [B("o0"), B("o1")]
    b_m = [B("m") for _ in range(8)]
    b_sq = [B("sq0"), B("sq1")]
    b_lnv, b_rstd = B("lnv"), B("rstd")
    b_tmp = [B("t0"), B("t1")]
    b_hn = [B("hn") for _ in range(8)]
    b_h = [B("h") for _ in range(NHC)]
    b_sl = [B("sl0"), B("sl1")]
    b_wg = [B("wg0"), B("wg1")]
    b_wu = [B("wu0"), B("wu1")]
    b_wd = [B("wd0"), B("wd1")]
    b_pm = [b_bank[0], b_bank[1]]
    b_pss = b_bank[2]
    b_pg = [b_bank[3], b_bank[4]]
    b_pu = [b_bank[5], b_bank[6]]
    b_y = [B("y0"), B("y1")]

    wov = w_out.rearrange("(c p) n -> p c n", p=128)
    wgv = w_g.rearrange("(c p) n -> p c n", p=128)
    wuv = w_u.rearrange("(c p) n -> p c n", p=128)
    wdv = w_d.rearrange("(c p) n -> p c n", p=128)

    S.op("sp", lambda e: e.dma_start(out=g_sb[:], in_=gains), writes=[b_g], dma_key="misc")
    S.op("dve", lambda e: e.memset(ones_bf[:], 1.0), writes=[b_ones])
    for half in range(2):
        S.op("pool", lambda e, half=half: e.dma_start(out=wo_bf[:, :, half * 512:(half + 1) * 512],
                                                      in_=wov[:, :, half * 512:(half + 1) * 512]),
             writes=[b_wo], dma_key="wo")

    groups = []
    c0 = 0
    while c0 < NHC:
        n = min(4, NHC - c0)
        groups.append((c0, n))
        c0 += n

    gcount = [0]

    def load_group(gi):
        c0, n = groups[gi]
        s = gcount[0] % 2
        gcount[0] += 1
        S.op("pool", lambda e: e.dma_start(out=wg_b[s][:, :, 0:n * 128], in_=wgv[:, :, c0 * 128:(c0 + n) * 128]),
             writes=[b_wg[s]], dma_key=f"wg{s}")
        S.op("pool", lambda e: e.dma_start(out=wu_b[s][:, :, 0:n * 128], in_=wuv[:, :, c0 * 128:(c0 + n) * 128]),
             writes=[b_wu[s]], dma_key=f"wu{s}")
        return s

    def load_wd(half):
        for (a, b) in ((0, 8), (8, 16), (16, NHC)):
            S.op("pool", lambda e, a=a, b=b: e.dma_start(out=wd_b[half][:, a:b, :],
                                                         in_=wdv[:, a:b, half * 512:(half + 1) * 512]),
                 writes=[b_wd[half]], dma_key=f"wd{half}")

    def load_tile(tt):
        s = tt % 2
        t0 = tt * NT
        S.op("sp", lambda e: e.dma_start(out=x_sb[s][:], in_=x_src(e, t0)),
             reads=b_in, writes=b_x[s], dma_key=f"x{s}")
        S.op("sp", lambda e: e.dma_start(out=o_bf[s][:], in_=o_src(e, t0)),
             reads=b_in, writes=[b_o[s]], dma_key=f"o{s}")

    def norm_stats_from(src_ap_fn, src_bufs_fn, oc):
        q = oc % 2
        S.op("act", lambda e: e.activation(out=sq[q][:], in_=src_ap_fn(oc), func=AF.Square),
             reads=src_bufs_fn(oc), writes=[b_sq[q]])
        S.op("pe", lambda e: e.matmul(pss[:], lhsT=ones_bf[:], rhs=sq[q][:], start=(oc == 0), stop=(oc == 7)),
             reads=[b_ones, b_sq[q]], writes=[b_pss])

    def make_rstd():
        S.op("act", lambda e: e.activation(out=lnv[:], in_=pss[:], func=AF.Ln, bias=EPS, scale=1.0 / D),
             reads=[b_pss], writes=[b_lnv])
        S.op("act", lambda e: e.activation(out=rstd[:], in_=lnv[:], func=AF.Exp, scale=-0.5),
             reads=[b_lnv], writes=[b_rstd])

    def residual_add(s, gj):
        for oc in range(8):
            q = oc % 2
            S.op("dve", lambda e, oc=oc, q=q: e.scalar_tensor_tensor(
                out=tmp[q][:], in0=m_sb[:, oc, :], scalar=g_sb[:, gj * 8 + oc:gj * 8 + oc + 1], in1=rstd[:],
                op0=ALU.mult, op1=ALU.mult), reads=[b_m[oc], b_g, b_rstd], writes=[b_tmp[q]])
            S.op("dve", lambda e, oc=oc, q=q: e.tensor_tensor(
                out=x_sb[s][:, oc, :], in0=x_sb[s][:, oc, :], in1=tmp[q][:], op=ALU.add),
                reads=[b_tmp[q], b_x[s][oc]], writes=[b_x[s][oc]])

    load_tile(0)

    def tile_body(tt):
        s = tt % 2
        if tt + 1 < ntile:
            load_tile(tt + 1)
        for oc in range(8):
            p = oc % 2
            for kc in range(8):
                S.op("pe", lambda e, oc=oc, kc=kc, p=p: e.matmul(
                    pm[p][:], lhsT=wo_bf[:, kc, oc * 128:(oc + 1) * 128], rhs=o_bf[s][:, kc, :],
                    start=(kc == 0), stop=(kc == 7)), reads=[b_wo, b_o[s]], writes=[b_pm[p]])
            S.op("dve", lambda e, oc=oc, p=p: e.tensor_copy(out=m_sb[:, oc, :], in_=pm[p][:]),
                 reads=[b_pm[p]], writes=[b_m[oc]])
            norm_stats_from(lambda oc_, p=p: pm[p][:], lambda oc_, p=p: [b_pm[p]], oc)
        make_rstd()
        residual_add(s, 0)
        for oc in range(8):
            norm_stats_from(lambda oc_: x_sb[s][:, oc_, :], lambda oc_: [b_x[s][oc_]], oc)
        make_rstd()
        for oc in range(8):
            S.op("dve", lambda e, oc=oc: e.scalar_tensor_tensor(
                out=hn[:, oc, :], in0=x_sb[s][:, oc, :], scalar=g_sb[:, 8 + oc:8 + oc + 1], in1=rstd[:],
                op0=ALU.mult, op1=ALU.mult), reads=[b_x[s][oc], b_g, b_rstd], writes=[b_hn[oc]])
        slot = load_group(0)
        for gi, (c0, n) in enumerate(groups):
            cur = slot
            if gi + 1 < len(groups):
                slot = load_group(gi + 1)
            elif True:
                load_wd(0)
                load_wd(1)
            for j in range(n):
                hc = c0 + j
                p = hc % 2
                for kc in range(8):
                    S.op("pe", lambda e, j=j, kc=kc, p=p, cur=cur: e.matmul(
                        pg[p][:], lhsT=wg_b[cur][:, kc, j * 128:(j + 1) * 128], rhs=hn[:, kc, :],
                        start=(kc == 0), stop=(kc == 7)), reads=[b_wg[cur], b_hn[kc]], writes=[b_pg[p]])
                for kc in range(8):
                    S.op("pe", lambda e, j=j, kc=kc, p=p, cur=cur: e.matmul(
                        pu[p][:], lhsT=wu_b[cur][:, kc, j * 128:(j + 1) * 128], rhs=hn[:, kc, :],
                        start=(kc == 0), stop=(kc == 7)), reads=[b_wu[cur], b_hn[kc]], writes=[b_pu[p]])
                S.op("act", lambda e, p=p: e.activation(out=sl[p][:], in_=pg[p][:], func=AF.Silu),
                     reads=[b_pg[p]], writes=[b_sl[p]])
                S.op("dve", lambda e, p=p, hc=hc: e.tensor_tensor(
                    out=h[:, hc, :], in0=sl[p][:], in1=pu[p][:], op=ALU.mult),
                    reads=[b_sl[p], b_pu[p]], writes=[b_h[hc]])
        for oc in range(8):
            p = oc % 2
            half, j = oc // 4, oc % 4
            for hc in range(NHC):
                S.op("pe", lambda e, hc=hc, p=p, half=half, j=j: e.matmul(
                    pm[p][:], lhsT=wd_b[half][:, hc, j * 128:(j + 1) * 128], rhs=h[:, hc, :],
                    start=(hc == 0), stop=(hc == NHC - 1)), reads=[b_wd[half], b_h[hc]], writes=[b_pm[p]])
            S.op("dve", lambda e, oc=oc, p=p: e.tensor_copy(out=m_sb[:, oc, :], in_=pm[p][:]),
                 reads=[b_pm[p]], writes=[b_m[oc]])
            norm_stats_from(lambda oc_, p=p: pm[p][:], lambda oc_, p=p: [b_pm[p]], oc)
        make_rstd()
        residual_add(s, 2)
        t0 = tt * NT
        S.op("sp", lambda e, s=s, t0=t0: e.dma_start(out=x_dst(e, t0), in_=x_sb[s][:]),
             reads=b_x[s], writes=[b_y[s]] + b_outs, dma_key=f"st{s}")
        if hn_dst is not None:
            for oc in range(8):
                norm_stats_from(lambda oc_: x_sb[s][:, oc_, :], lambda oc_: [b_x[s][oc_]], oc)
            make_rstd()
            for oc in range(8):
                S.op("dve", lambda e, oc=oc: e.scalar_tensor_tensor(
                    out=hn[:, oc, :], in0=x_sb[s][:, oc, :], scalar=g_sb[:, 24 + oc:24 + oc + 1], in1=rstd[:],
                    op0=ALU.mult, op1=ALU.mult), reads=[b_x[s][oc], b_g, b_rstd], writes=[b_hn[oc]])
            S.op("sp", lambda e, t0=t0: e.dma_start(out=hn_dst(e, t0), in_=hn[:]),
                 reads=b_hn, writes=[b_y[s]] + b_outs, dma_key=f"sh{s}")
    for tt in range(ntile):
        tile_body(tt)
    return b_y


def emit_attn(cx, mode, S, io, layer0, b_in, b_outs):
    nc, Sx, bank, b_bank = cx.nc, cx.S, cx.bank, cx.b_bank
    cx.arena.reset()
    sb = lambda name, shape, dt: cx.arena.alloc(shape, dt)
    op = Sx.op
    ntile = S // NT
    nkb = S // 128
    KW = 64 if mode == "C" else 512
    rope = mode in ("A", "C")
    oT = io["oT"]

    g_sb = sb("g_sb", [128, 8], F32)
    rc_sb = sb("rc_sb", [128, 4], F32)
    ex_sb = sb("ex_sb", [128, 264], F32)
    cm_bf = sb("cm_bf", [128, 256], BF16)
    ones_bf = sb("ones_bf", [128, 128], BF16)
    wq_bf = sb("wq_bf", [128, 8, 512], BF16)
    KWk = 128 if mode == "C" else 512
    wk_bf = sb("wk_bf", [128, 8, KWk], BF16)
    wv_bf = sb("wv_bf", [128, 8, KW], BF16)
    b_g, b_rc, b_ex, b_cm, b_ones, b_wq, b_wk, b_wv = [Buf(n) for n in "g rc ex cm ones wq wk wv".split()]
    op("sp", lambda e: e.dma_start(out=g_sb[:], in_=io["g0"]), writes=[b_g], dma_key="c0")
    op("sp", lambda e: e.dma_start(out=rc_sb[:], in_=io["rc"]), writes=[b_rc], dma_key="c1")
    op("sp", lambda e: e.dma_start(out=ex_sb[:], in_=io["extra"]), writes=[b_ex], dma_key="c2")
    op("pool", lambda e: e.dma_start(out=cm_bf[:], in_=io["cmat"]), writes=[b_cm], dma_key="c3")
    op("dve", lambda e: e.memset(ones_bf[:], 1.0), writes=[b_ones])
    wqv = io["wq"].rearrange("(c p) n -> p c n", p=128)
    wkv = io["wk"].rearrange("(c p) n -> p c n", p=128)
    wvv = io["wv"].rearrange("(c p) n -> p c n", p=128)
    op("pool", lambda e: e.dma_start(out=wq_bf[:], in_=wqv), writes=[b_wq], dma_key="w0")
    if mode == "C":
        op("pool", lambda e: e.dma_start(out=wk_bf[:, :, 0:64], in_=wkv), writes=[b_wk], dma_key="w1")
        op("pool", lambda e: e.dma_start(out=wk_bf[:, :, 64:128], in_=wkv), writes=[b_wk], dma_key="w1")
    else:
        op("pool", lambda e: e.dma_start(out=wk_bf[:], in_=wkv), writes=[b_wk], dma_key="w1")
    op("pool", lambda e: e.dma_start(out=wv_bf[:], in_=wvv), writes=[b_wv], dma_key="w2")

    if rope:
        Ctab = sb("Ctab", [128, NT], F32)
        Stab = sb("Stab", [128, NT], F32)
        pos_i = sb("pos_i", [128, NT], I32)
        ang = sb("ang", [128, NT], F32)
        tr = sb("tr", [128, NT], F32)
        tr2 = sb("tr2", [128, NT], F32)
        b_tr2 = Buf("tr2")
        b_C = [Buf("C")] * ntile
        b_S = [Buf("S")] * ntile
        b_pos, b_ang, b_tr = Buf("pos"), Buf("ang"), Buf("tr")
        negpi = sb("negpi", [128, 1], F32)
        b_negpi = Buf("negpi")
        op("dve", lambda e: e.memset(negpi[:], -PI), writes=[b_negpi])
    def rope_tables(tt):
        if True:
            sl_ = slice(tt * NT, (tt + 1) * NT)
            op("sp", lambda e, sl_=sl_: e.dma_start(out=pos_i[:], in_=io["pos"][:, sl_]), writes=[b_pos], dma_key="pos")
            op("dve", lambda e: e.tensor_copy(out=ang[:], in_=pos_i[:]), reads=[b_pos], writes=[b_ang])
            op("dve", lambda e: e.tensor_scalar(out=ang[:], in0=ang[:], scalar1=rc_sb[:, 0:1], scalar2=None, op0=ALU.mult),
               reads=[b_ang, b_rc], writes=[b_ang])
            for (tab, b_tab, shift, sgn) in ((Stab, b_S, 0.0, True), (Ctab, b_C, 0.5 * PI, False)):
                op("dve", lambda e, shift=shift: e.tensor_scalar(out=tr[:], in0=ang[:], scalar1=shift, scalar2=1.0 / (2 * PI),
                                                                 op0=ALU.add, op1=ALU.mult), reads=[b_ang], writes=[b_tr])
                op("dve", lambda e: e.tensor_copy(out=pos_i[:], in_=tr[:]), reads=[b_tr], writes=[b_pos])
                op("dve", lambda e: e.tensor_copy(out=tr[:], in_=pos_i[:]), reads=[b_pos], writes=[b_tr])
                op("dve", lambda e: e.tensor_scalar(out=tr[:], in0=tr[:], scalar1=-2 * PI, scalar2=None, op0=ALU.mult),
                   reads=[b_tr], writes=[b_tr])
                op("dve", lambda e, shift=shift: e.scalar_tensor_tensor(out=tr[:], in0=ang[:], scalar=shift, in1=tr[:],
                                                                        op0=ALU.add, op1=ALU.add), reads=[b_ang, b_tr], writes=[b_tr])
                op("dve", lambda e: e.tensor_scalar(out=tr2[:], in0=tr[:], scalar1=PI, scalar2=-2 * PI, op0=ALU.is_gt, op1=ALU.mult),
                   reads=[b_tr], writes=[b_tr2])
                op("dve", lambda e: e.tensor_tensor(out=tr[:], in0=tr[:], in1=tr2[:], op=ALU.add), reads=[b_tr, b_tr2], writes=[b_tr])
                op("act", lambda e, sl_=sl_, tab=tab: e.activation(out=tab[:], in_=tr[:], func=AF.Sin),
                   reads=[b_tr], writes=[b_tab[tt]])
                if sgn:
                    op("dve", lambda e, sl_=sl_, tab=tab: e.tensor_scalar(out=tab[:], in0=tab[:], scalar1=rc_sb[:, 1:2],
                                                                          scalar2=None, op0=ALU.mult),
                       reads=[b_tab[tt], b_rc], writes=[b_tab[tt]])

    nqc = 4
    nkc = 1 if mode == "C" else 4
    qT = sb("qT", [128, nqc, S], BF16)
    kT = sb("kT", [128, nkc, S], BF16)
    v_sb = sb("v_sb", [128, nkb, KW], BF16)
    b_q = [[Buf("q") for _ in range(ntile)] for _ in range(nqc)]
    b_k = [[Buf("k") for _ in range(ntile)] for _ in range(nkc)]
    b_v = [Buf("v") for _ in range(nkb)]
    x_sb = sb("x_sb", [128, 8, NT], F32)
    hn = sb("hn", [128, 8, NT], BF16)
    sq = [sb(f"sq{i}", [128, NT], BF16) for i in range(2)]
    lnv = sb("lnv", [128, NT], F32)
    rstd = sb("rstd", [128, NT], F32)
    qb = [sb(f"qb{i}", [128, NT], BF16) for i in range(2)]
    t1 = [sb(f"t1{i}", [128, NT], F32) for i in range(2)]
    t2 = [sb(f"t2{i}", [128, NT], F32) for i in range(2)]
    b_x = [Buf("x") for _ in range(8)]
    b_hn = [Buf("hn") for _ in range(8)]
    b_sq = [Buf("sq0"), Buf("sq1")]
    b_lnv, b_rstd = Buf("lnv"), Buf("rstd")
    b_qb = [Buf("qb0"), Buf("qb1")]
    b_t1 = [Buf("t10"), Buf("t11")]
    b_t2 = [Buf("t20"), Buf("t21")]
    PP = [0, 1]
    PSW = [2, 3]
    PSS = 4
    cnt = [0]

    def proj_feat(tt, w_bf, b_w, c, npart, dstT, b_dst, do_rope, scale):
        sl_ = slice(tt * NT, (tt + 1) * NT)
        i = cnt[0] % 2
        cnt[0] += 1
        pb, sw = PP[i], PSW[i]
        for kc in range(8):
            op("pe", lambda e, kc=kc: e.matmul(bank[pb][0:npart, :], lhsT=w_bf[:, kc, c * 128:c * 128 + npart], rhs=hn[:, kc, :],
                                               start=(kc == 0), stop=(kc == 7)), reads=[b_w, b_hn[kc]], writes=[b_bank[pb]])
        if not do_rope:
            if scale == 1.0:
                op("act", lambda e: e.copy(out=dstT[0:npart, c, sl_], in_=bank[pb][0:npart, :]), reads=[b_bank[pb]], writes=[b_dst])
            else:
                op("dve", lambda e: e.tensor_scalar(out=dstT[0:npart, c, sl_], in0=bank[pb][0:npart, :], scalar1=scale, scalar2=None,
                                                    op0=ALU.mult), reads=[b_bank[pb]], writes=[b_dst])
            return
        op("act", lambda e: e.copy(out=qb[i][0:npart, :], in_=bank[pb][0:npart, :]), reads=[b_bank[pb]], writes=[b_qb[i]])
        op("pe", lambda e: e.matmul(bank[sw][0:npart, :], lhsT=cm_bf[0:npart, 0:npart], rhs=qb[i][0:npart, :], start=True, stop=True),
           reads=[b_cm, b_qb[i]], writes=[b_bank[sw]])
        op("dve", lambda e: e.tensor_tensor(out=t1[i][0:npart, :], in0=bank[pb][0:npart, :], in1=Ctab[0:npart, :], op=ALU.mult),
           reads=[b_bank[pb], b_C[tt]], writes=[b_t1[i]])
        op("dve", lambda e: e.tensor_tensor(out=t2[i][0:npart, :], in0=bank[sw][0:npart, :], in1=Stab[0:npart, :], op=ALU.mult),
           reads=[b_bank[sw], b_S[tt]], writes=[b_t2[i]])
        op("pool", lambda e: e.tensor_tensor(out=dstT[0:npart, c, sl_], in0=t1[i][0:npart, :], in1=t2[i][0:npart, :], op=ALU.add),
           reads=[b_t1[i], b_t2[i]], writes=[b_dst])

    def proj_tile(tt):
        sl_ = slice(tt * NT, (tt + 1) * NT)
        if rope:
            rope_tables(tt)
        if not layer0:
            op("sp", lambda e: e.dma_start(out=hn[:], in_=io["hn_src"](e, tt)), reads=b_in, writes=b_hn, dma_key="x")
        else:
            op("sp", lambda e: e.dma_start(out=x_sb[:], in_=io["x_src"](e, tt)), reads=b_in, writes=b_x, dma_key="x")
        for kc in range(8 if layer0 else 0):
            q_ = kc % 2
            op("act", lambda e, kc=kc, q_=q_: e.activation(out=sq[q_][:], in_=x_sb[:, kc, :], func=AF.Square),
               reads=[b_x[kc]], writes=[b_sq[q_]])
            op("pe", lambda e, kc=kc, q_=q_: e.matmul(bank[PSS][:], lhsT=ones_bf[:], rhs=sq[q_][:], start=(kc == 0), stop=(kc == 7)),
               reads=[b_ones, b_sq[q_]], writes=[b_bank[PSS]])
        if layer0:
            op("act", lambda e: e.activation(out=lnv[:], in_=bank[PSS][:], func=AF.Ln, bias=EPS, scale=1.0 / D),
               reads=[b_bank[PSS]], writes=[b_lnv])
            op("act", lambda e: e.activation(out=rstd[:], in_=lnv[:], func=AF.Exp, scale=-0.5), reads=[b_lnv], writes=[b_rstd])
        for kc in range(8 if layer0 else 0):
            op("dve", lambda e, kc=kc: e.scalar_tensor_tensor(out=hn[:, kc, :], in0=x_sb[:, kc, :], scalar=g_sb[:, kc:kc + 1], in1=rstd[:],
                                                             op0=ALU.mult, op1=ALU.mult),
               reads=[b_x[kc], b_g, b_rstd], writes=[b_hn[kc]])
        for c in range(nqc):
            proj_feat(tt, wq_bf, b_wq, c, 128, qT, b_q[c][tt], rope, 1.0)
        for c in range(nkc):
            proj_feat(tt, wk_bf, b_wk, c, 128, kT, b_k[c][tt], rope, 0.125 if mode == "B" else 1.0)
        for sub in range(4):
            i = cnt[0] % 2
            cnt[0] += 1
            pb = PP[i]
            kbi = tt * 4 + sub
            for kc in range(8):
                op("pe", lambda e, kc=kc, sub=sub, pb=pb: e.matmul(bank[pb][:, 0:KW], lhsT=hn[:, kc, sub * 128:(sub + 1) * 128],
                                                                   rhs=wv_bf[:, kc, :], start=(kc == 0), stop=(kc == 7)),
                   reads=[b_wv, b_hn[kc]], writes=[b_bank[pb]])
            op("act", lambda e, kbi=kbi, pb=pb: e.copy(out=v_sb[:, kbi, :], in_=bank[pb][:, 0:KW]), reads=[b_bank[pb]], writes=[b_v[kbi]])

    for tt in range(ntile):
        proj_tile(tt)

    b_out = [Buf("out0"), Buf("out1")]
    if mode == "A":
        lp = ex_sb
        lt = sb("lt", [128, 128], F32)
        ls = sb("ls", [128, 8], F32)
        b_lt, b_ls = Buf("lt"), Buf("ls")
        op("dve", lambda e: e.tensor_tensor(out=lt[:, 0:64], in0=lp[:, 0:64], in1=lp[:, 64:128], op=ALU.mult), reads=[b_ex], writes=[b_lt])
        op("dve", lambda e: e.tensor_tensor(out=lt[:, 64:128], in0=lp[:, 128:192], in1=lp[:, 192:256], op=ALU.mult), reads=[b_ex, b_lt], writes=[b_lt])
        op("dve", lambda e: e.tensor_reduce(out=ls[:, 0:1], in_=lt[:, 0:64], axis=AX.X, op=ALU.add), reads=[b_lt], writes=[b_ls])
        op("dve", lambda e: e.tensor_reduce(out=ls[:, 1:2], in_=lt[:, 64:128], axis=AX.X, op=ALU.add), reads=[b_lt, b_ls], writes=[b_ls])
        op("act", lambda e: e.activation(out=ls[:, 2:4], in_=ls[:, 0:2], func=AF.Exp), reads=[b_ls], writes=[b_ls])
        op("dve", lambda e: e.tensor_tensor(out=ls[:, 4:5], in0=ls[:, 2:3], in1=ls[:, 3:4], op=ALU.subtract), reads=[b_ls], writes=[b_ls])
        op("dve", lambda e: e.tensor_scalar(out=ls[:, 5:6], in0=ls[:, 4:5], scalar1=lp[:, 257:258], scalar2=-1.0, op0=ALU.add, op1=ALU.mult),
           reads=[b_ls, b_ex], writes=[b_ls])
        op("dve", lambda e: e.tensor_tensor(out=ls[:, 6:7], in0=lp[:, 256:257], in1=lp[:, 258:259], op=ALU.mult), reads=[b_ls, b_ex], writes=[b_ls])
        pT = [sb(f"pT{i}", [128, NT], BF16) for i in range(4)]
        b_pT = [Buf(f"pT{i}") for i in range(4)]
        r0 = sb("r0", [128, NT], F32)
        ta = [sb(f"ta{i}", [128, NT], F32) for i in range(2)]
        ob = sb("ob", [128, NT], F32)
        osq = sb("osq", [128, NT], BF16)
        of = [sb(f"of{i}", [128, NT], BF16) for i in range(2)]
        b_r0, b_ob, b_osq = Buf("r0"), Buf("ob"), Buf("osq")
        b_ta = [Buf("ta0"), Buf("ta1")]
        b_of = [Buf("of0"), Buf("of1")]
        SC = [0, 1, 4]
        NUM = [2, 3]
        DEN = [5, 6]
        SSB = 7
        ocnt = [0]
        LA = 2

        def a_steps(h, qt):
            last = 4 * qt + 3
            steps = [(kb, c) for kb in range(last + 1) for c in range(2)]
            n = len(steps)

            def geom(i):
                kb, c = steps[i]
                j = kb - 4 * qt
                col0 = j * 128 if j > 0 else 0
                return kb, c, j, col0, SC[i % 3], i % 4, slice(c * 64, (c + 1) * 64)

            def QK(i):
                kb, c, j, col0, sc, r, ps_ = geom(i)
                op("pe", lambda e: e.matmul(bank[sc][:, col0:NT], lhsT=kT[ps_, h, kb * 128:(kb + 1) * 128],
                                            rhs=qT[ps_, h, qt * NT + col0:(qt + 1) * NT], start=True, stop=True),
                   reads=[b_k[h][kb // 4], b_q[h][qt]], writes=[b_bank[sc]])

            def EX(i):
                kb, c, j, col0, sc, r, ps_ = geom(i)
                op("act", lambda e: e.activation(out=pT[r][:, col0:NT], in_=bank[sc][:, col0:NT], func=AF.Exp, scale=0.125),
                   reads=[b_bank[sc]], writes=[b_pT[r]])
                if j >= 0:
                    op("pool", lambda e: e.affine_select(out=pT[r][:, col0:col0 + 128], in_=pT[r][:, col0:col0 + 128], pattern=[[1, 128]],
                                                         compare_op=ALU.is_ge, fill=0.0, base=0, channel_multiplier=-1),
                       reads=[b_pT[r]], writes=[b_pT[r]])

            def PVD(i):
                kb, c, j, col0, sc, r, ps_ = geom(i)
                op("pe", lambda e: e.matmul(bank[NUM[c]][:, col0:NT], lhsT=v_sb[:, kb, h * 128:(h + 1) * 128], rhs=pT[r][:, col0:NT],
                                            start=(kb == 0), stop=(kb == last)), reads=[b_v[kb], b_pT[r]], writes=[b_bank[NUM[c]]])
                op("pe", lambda e: e.matmul(bank[DEN[c]][:, col0:NT], lhsT=ones_bf[:], rhs=pT[r][:, col0:NT],
                                            start=(kb == 0), stop=(kb == last)), reads=[b_ones, b_pT[r]], writes=[b_bank[DEN[c]]])

            for i in range(min(LA, n)):
                QK(i)
            for i in range(n):
                EX(i)
                if i + LA < n:
                    QK(i + LA)
                PVD(i)

        def a_final(h, qt):
            for c in range(2):
                op("act", lambda e, c=c: e.activation(out=r0[:], in_=bank[DEN[c]][:], func=AF.Ln), reads=[b_bank[DEN[c]]], writes=[b_r0])
                op("act", lambda e: e.activation(out=r0[:], in_=r0[:], func=AF.Exp, scale=-1.0), reads=[b_r0], writes=[b_r0])
                op("dve", lambda e, c=c: e.tensor_tensor(out=ta[c][:], in0=bank[NUM[c]][:], in1=r0[:], op=ALU.mult),
                   reads=[b_bank[NUM[c]], b_r0], writes=[b_ta[c]])
            op("dve", lambda e: e.scalar_tensor_tensor(out=ob[:], in0=ta[1][:], scalar=ls[:, 5:6], in1=ta[0][:], op0=ALU.mult, op1=ALU.add),
               reads=[b_ta[0], b_ta[1], b_ls], writes=[b_ob])
            op("act", lambda e: e.activation(out=osq[:], in_=ob[:], func=AF.Square), reads=[b_ob], writes=[b_osq])
            op("pe", lambda e: e.matmul(bank[SSB][:], lhsT=ones_bf[:], rhs=osq[:], start=True, stop=True),
               reads=[b_ones, b_osq], writes=[b_bank[SSB]])
            op("act", lambda e: e.activation(out=lnv[:], in_=bank[SSB][:], func=AF.Ln, bias=EPS, scale=1.0 / 128),
               reads=[b_bank[SSB]], writes=[b_lnv])
            op("act", lambda e: e.activation(out=rstd[:], in_=lnv[:], func=AF.Exp, scale=-0.5), reads=[b_lnv], writes=[b_rstd])
            oi = ocnt[0] % 2
            ocnt[0] += 1
            op("dve", lambda e, oi=oi: e.scalar_tensor_tensor(out=of[oi][:], in0=ob[:], scalar=ls[:, 6:7], in1=rstd[:],
                                                             op0=ALU.mult, op1=ALU.mult),
               reads=[b_ob, b_ls, b_rstd], writes=[b_of[oi]])
            op("sp", lambda e, oi=oi, h=h, qt=qt: e.dma_start(out=oT[h * 128:(h + 1) * 128, qt * NT:(qt + 1) * NT], in_=of[oi][:]),
               reads=[b_of[oi]], writes=[b_out[oi]] + b_outs, dma_key=f"out{oi}")

        for h in range(4):
            for qt in range(ntile):
                a_steps(h, qt)
                a_final(h, qt)

    if mode == "B":
        e_sb = [sb(f"e_sb{i}", [128, NT], F32) for i in range(3)]
        L_sb = [sb(f"L_sb{i}", [128, NT], BF16) for i in range(3)]
        a_sb = [sb(f"a_sb{i}", [128, NT], BF16) for i in range(3)]
        aarg = [sb(f"aarg{i}", [128, NT], F32) for i in range(2)]
        carry = sb("carry", [128, NT], F32)
        ofb = [sb(f"ofb{i}", [64, NT], BF16) for i in range(2)]
        b_e = [Buf("e") for _ in range(3)]
        b_L = [Buf("L") for _ in range(3)]
        b_a = [Buf("a") for _ in range(3)]
        b_aarg = [Buf("aarg") for _ in range(2)]
        b_carry = Buf("carry")
        b_ofb = [Buf("ofb0"), Buf("ofb1")]
        ARG = [0, 1, 2]
        CS = [3, 4]
        OPS = [5, 6]
        gq = [0]
        def b_group(h, qt):
            cch, ps_ = h // 2, slice((h % 2) * 64, (h % 2) * 64 + 64)
            blocks = list(range(4 * qt + 3, -1, -1))
            n = len(blocks)
            ob_ = OPS[gq[0] % 2]
            oi = gq[0] % 2
            gq[0] += 1

            def diag_mask(i, tile_, b_t):
                j = blocks[i] - 4 * qt
                if j >= 0:
                    op("pool", lambda e, j=j: e.affine_select(out=tile_[:], in_=tile_[:], pattern=[[1, NT]], compare_op=ALU.is_gt,
                                                              fill=0.0, base=-j * 128, channel_multiplier=-1),
                       reads=[b_t], writes=[b_t])

            def Z(i):
                kb = blocks[i]
                ab = ARG[i % 3]
                op("pe", lambda e: e.matmul(bank[ab][:], lhsT=kT[ps_, cch, kb * 128:(kb + 1) * 128], rhs=qT[ps_, cch, qt * NT:(qt + 1) * NT],
                                            start=True, stop=True), reads=[b_k[cch][kb // 4], b_q[cch][qt]], writes=[b_bank[ab]])

            def EL(i):
                ab, r = ARG[i % 3], i % 3
                op("act", lambda e: e.activation(out=e_sb[r][:], in_=bank[ab][:], func=AF.Exp), reads=[b_bank[ab]], writes=[b_e[r]])
                op("act", lambda e: e.activation(out=L_sb[r][:], in_=e_sb[r][:], func=AF.Ln, bias=1.0), reads=[b_e[r]], writes=[b_L[r]])
                diag_mask(i, L_sb[r], b_L[r])

            def CUM(i):
                ab, r, cb = ARG[i % 3], i % 3, CS[i % 2]
                op("pe", lambda e: e.matmul(bank[ab][:], lhsT=cm_bf[:, 0:128], rhs=L_sb[r][:], start=False, stop=True, skip_group_check=True),
                   reads=[b_cm, b_L[r]], writes=[b_bank[ab]])
                op("pe", lambda e: e.matmul(bank[cb][:], lhsT=ones_bf[:], rhs=L_sb[r][:], start=True, stop=True),
                   reads=[b_ones, b_L[r]], writes=[b_bank[cb]])

            def SUB(i):
                ab, cb, q_ = ARG[i % 3], CS[i % 2], i % 2
                if i == 0:
                    op("dve", lambda e: e.tensor_copy(out=aarg[q_][:], in_=bank[ab][:]), reads=[b_bank[ab]], writes=[b_aarg[q_]])
                    op("dve", lambda e: e.tensor_copy(out=carry[:], in_=bank[cb][:]), reads=[b_bank[cb]], writes=[b_carry])
                else:
                    op("dve", lambda e: e.tensor_tensor(out=aarg[q_][:], in0=bank[ab][:], in1=carry[:], op=ALU.subtract),
                       reads=[b_bank[ab], b_carry], writes=[b_aarg[q_]])
                    op("dve", lambda e: e.tensor_tensor(out=carry[:], in0=bank[cb][:], in1=carry[:], op=ALU.add),
                       reads=[b_bank[cb], b_carry], writes=[b_carry])

            def AA(i):
                r, q_ = i % 3, i % 2
                op("act", lambda e: e.activation(out=a_sb[r][:], in_=aarg[q_][:], func=AF.Exp), reads=[b_aarg[q_]], writes=[b_a[r]])
                diag_mask(i, a_sb[r], b_a[r])

            def PV(i):
                kb, r = blocks[i], i % 3
                op("pe", lambda e: e.matmul(bank[ob_][0:64, :], lhsT=v_sb[:, kb, h * 64:(h + 1) * 64], rhs=a_sb[r][:],
                                            start=(i == 0), stop=(i == n - 1)), reads=[b_v[kb], b_a[r]], writes=[b_bank[ob_]])

            Z(0)
            EL(0)
            if n > 1:
                Z(1)
            for i in range(n):
                CUM(i)
                if i + 1 < n:
                    EL(i + 1)
                if i + 2 < n:
                    Z(i + 2)
                SUB(i)
                AA(i)
                PV(i)
            op("dve", lambda e: e.tensor_copy(out=ofb[oi][:], in_=bank[ob_][0:64, :]), reads=[b_bank[ob_]], writes=[b_ofb[oi]])
            op("sp", lambda e: e.dma_start(out=oT[h * 64:(h + 1) * 64, qt * NT:(qt + 1) * NT], in_=ofb[oi][:]),
               reads=[b_ofb[oi]], writes=[b_out[oi]] + b_outs, dma_key=f"out{oi}")

        for h in range(8):
            for qt in range(ntile):
                b_group(h, qt)

    if mode == "C":
        esk = sb("esk", [128, 8], F32)
        zer = sb("zer", [64, 128], F32)
        esk_t = [sb(f"esk_t{i}", [64, NT], F32) for i in range(2)]
        p_sb = [sb(f"p_sb{i}", [128, NT], BF16) for i in range(4)]
        dn = sb("dn", [64, NT], F32)
        ofc = [sb(f"ofc{i}", [64, NT], BF16) for i in range(2)]
        b_esk, b_zer, b_dn = Buf("esk"), Buf("zer"), Buf("dn")
        b_eskt = [Buf("eskt0"), Buf("eskt1")]
        b_p = [Buf(f"p{i}") for i in range(4)]
        b_ofc = [Buf("ofc0"), Buf("ofc1")]
        op("act", lambda e: e.activation(out=esk[:], in_=ex_sb[:, 0:8], func=AF.Exp), reads=[b_ex], writes=[b_esk])
        op("dve", lambda e: e.memset(zer[:], 0.0), writes=[b_zer])
        for hd in range(8):
            op("dve", lambda e, hd=hd: e.tensor_scalar(out=esk_t[hd % 2][:, (hd // 2) * 128:(hd // 2 + 1) * 128], in0=zer[:],
                                                       scalar1=esk[0:64, hd:hd + 1], scalar2=None, op0=ALU.add),
               reads=[b_zer, b_esk, b_eskt[hd % 2]], writes=[b_eskt[hd % 2]])
        oview = oT.rearrange("(h d) s -> d h s", d=64)
        SCB = [0, 1, 2, 3]
        OB = [4, 5]
        DB = [6, 7]
        pc = [0]
        gq = [0]

        def c_block(nb, hg):
            slots = ([(nb - 1, 0)] if nb > 0 else []) + [(nb, 1)]
            pr = []
            for (kb, slot) in slots:
                i = pc[0] % 4
                pc[0] += 1
                pr.append((kb, slot, i))
                for hh in range(4):
                    hd = hh * 2 + hg
                    ps_ = slice((hd % 2) * 64, (hd % 2) * 64 + 64)
                    op("pe", lambda e, hh=hh, hd=hd, ps_=ps_, kb=kb, i=i: e.matmul(
                        bank[SCB[i]][:, hh * 128:(hh + 1) * 128], lhsT=kT[ps_, 0, kb * 128:(kb + 1) * 128],
                        rhs=qT[ps_, hd // 2, nb * 128:(nb + 1) * 128], start=True, stop=True),
                        reads=[b_k[0][kb // 4], b_q[hd // 2][nb // 4]], writes=[b_bank[SCB[i]]])
                op("act", lambda e, i=i: e.activation(out=p_sb[i][:], in_=bank[SCB[i]][:], func=AF.Exp, scale=0.125),
                   reads=[b_bank[SCB[i]]], writes=[b_p[i]])
                for hh in range(4):
                    cs_ = slice(hh * 128, (hh + 1) * 128)
                    if slot == 1:
                        op("pool", lambda e, i=i, cs_=cs_: e.affine_select(out=p_sb[i][:, cs_], in_=p_sb[i][:, cs_], pattern=[[1, 128]],
                                                                          compare_op=ALU.is_ge, fill=0.0, base=0, channel_multiplier=-1),
                           reads=[b_p[i]], writes=[b_p[i]])
                    else:
                        op("pool", lambda e, i=i, cs_=cs_: e.affine_select(out=p_sb[i][:, cs_], in_=p_sb[i][:, cs_], pattern=[[-1, 128]],
                                                                          compare_op=ALU.is_gt, fill=0.0, base=0, channel_multiplier=1),
                           reads=[b_p[i]], writes=[b_p[i]])
            g = gq[0] % 2
            gq[0] += 1
            for hh in range(4):
                cs_ = slice(hh * 128, (hh + 1) * 128)
                for si, (kb, slot, i) in enumerate(pr):
                    op("pe", lambda e, cs_=cs_, kb=kb, i=i, si=si: e.matmul(bank[OB[g]][0:64, cs_], lhsT=v_sb[:, kb, 0:64], rhs=p_sb[i][:, cs_],
                                                                           start=(si == 0), stop=(si == len(pr) - 1)),
                       reads=[b_v[kb], b_p[i]], writes=[b_bank[OB[g]]])
                for si, (kb, slot, i) in enumerate(pr):
                    op("pe", lambda e, cs_=cs_, i=i, si=si: e.matmul(bank[DB[g]][0:64, cs_], lhsT=ones_bf[:, 0:64], rhs=p_sb[i][:, cs_],
                                                                    start=(si == 0), stop=(si == len(pr) - 1)),
                       reads=[b_ones, b_p[i]], writes=[b_bank[DB[g]]])
            op("dve", lambda e: e.tensor_tensor(out=dn[:], in0=bank[DB[g]][0:64, :], in1=esk_t[hg][:], op=ALU.add),
               reads=[b_bank[DB[g]], b_eskt[hg]], writes=[b_dn])
            op("dve", lambda e: e.reciprocal(out=dn[:], in_=dn[:]), reads=[b_dn], writes=[b_dn])
            op("dve", lambda e: e.tensor_tensor(out=ofc[g][:], in0=bank[OB[g]][0:64, :], in1=dn[:], op=ALU.mult),
               reads=[b_bank[OB[g]], b_dn], writes=[b_ofc[g]])
            for hh in range(4):
                hd = hh * 2 + hg
                op("sp", lambda e, hh=hh, hd=hd: e.dma_start(out=oT[hd * 64:(hd + 1) * 64, nb * 128:(nb + 1) * 128],
                                                             in_=ofc[g][:, hh * 128:(hh + 1) * 128]),
                   reads=[b_ofc[g]], writes=[b_out[g]] + b_outs, dma_key=f"out{g}")

        for nb in range(nkb):
            for hg in range(2):
                c_block(nb, hg)

    return b_out


class _Cx:
    pass


def fused_program(SEQ, DEPTH):
    nc = bass.Bass("TRN2", target_bir_lowering=False)
    HALF = SEQ // 2
    tph = HALF // NT
    dr = lambda n, sh, dt=F32, kind="ExternalInput": nc.dram_tensor(n, sh, dt, kind=kind).ap()
    xT = dr("xT", [D, SEQ])
    pos = dr("pos", [128, SEQ], I32)
    rc = dr("rc", [128, 4])
    cmP = dr("cmP", [128, 256])
    cmT = dr("cmT", [128, 256])
    g0 = dr("g0", [128, 8])
    L = []
    for i in range(DEPTH):
        KW = 64 if i % 3 == 2 else 512
        L.append(dict(wq=dr(f"wq{i}", [D, 512]), wk=dr(f"wk{i}", [D, KW]), wv=dr(f"wv{i}", [D, KW]),
                      extra=dr(f"extra{i}", [128, 264]), w_out=dr(f"w_out{i}", [D, D]), w_g=dr(f"w_g{i}", [D, HID]),
                      w_u=dr(f"w_u{i}", [D, HID]), w_d=dr(f"w_d{i}", [HID, D]), gains=dr(f"gains{i}", [128, 32])))
    yT = dr("yT", [D, HALF], F32, kind="ExternalOutput")
    o_src = [nc.dram_tensor(f"o_src{i}", [512, SEQ], BF16) for i in range(DEPTH)]
    og = [nc.dram_tensor(f"og{i}", [8 * 512, SEQ], BF16) for i in range(DEPTH)]
    hn_src = [nc.dram_tensor(f"hn_src{i}", [D, HALF], BF16) for i in range(DEPTH - 1)]
    hng = [nc.dram_tensor(f"hng{i}", [8 * D, HALF], BF16) for i in range(DEPTH - 1)]
    x_res = nc.dram_tensor("x_res", [D, HALF], F32)
    fm = lambda ap_: ap_.rearrange("(c p) t -> p c t", p=128)
    with ExitStack() as es:
        cx = _Cx()
        cx.nc = nc
        cx.S = Sched()
        cx.bank = [es.enter_context(nc.psum_tensor(f"bank{i}", [128, NT], F32)) for i in range(8)]
        cx.b_bank = [Buf(f"bank{i}", psum=True) for i in range(8)]
        cx.arena = Arena(nc, es, 206 * 1024)
        S = cx.S
        b_xres, b_y = Buf("xres"), Buf("y")
        b_hng = None
        for i in range(DEPTH):
            mode = "ABC"[i % 3]
            last = (i == DEPTH - 1)
            io = dict(g0=g0, wq=L[i]["wq"], wk=L[i]["wk"], wv=L[i]["wv"], cmat=(cmT if mode == "B" else cmP), rc=rc, pos=pos,
                      extra=L[i]["extra"], oT=o_src[i].ap())
            io["x_src"] = lambda e, tt: fm(xT)[:, :, tt * NT:(tt + 1) * NT]
            if i > 0:
                hsrc = hng[i - 1]

                def hn_src_fn(e, tt, hsrc=hsrc):
                    pid = PID(e)
                    base = (pid - pid % 2) * D + (tt // tph) * D
                    return fm(hsrc.ap()[bass.ds(base, D), (tt % tph) * NT:(tt % tph + 1) * NT])
                io["hn_src"] = hn_src_fn
            b_osrc, b_og = Buf("osrc"), Buf("og")
            emit_attn(cx, mode, SEQ, io, i == 0, [b_hng] if i > 0 else [], [b_osrc])
            cc = S.op("pool", lambda e, i=i: e.collective_compute("AllGather", ALU.bypass, replica_groups=[list(range(8))],
                                                                 ins=[o_src[i].ap().opt()], outs=[og[i].ap().opt()]),
                      reads=[b_osrc], writes=[b_og], dma_key=f"cc{2 * i}", sem_inc=1, after_all=True)
            S.barrier(cc)

            def o_src_fn(e, t0, i=i):
                pid = PID(e)
                par = pid % 2
                return fm(og[i].ap()[bass.ds((pid - par) * 512, D), bass.ds(par * HALF + t0, NT)])
            if i == 0:
                def x_src_fn(e, t0):
                    par = PID(e) % 2
                    return fm(xT[:, bass.ds(par * HALF + t0, NT)])
            else:
                def x_src_fn(e, t0):
                    return fm(x_res.ap())[:, :, t0:t0 + NT]
            if last:
                x_dst_fn = lambda e, t0: fm(yT)[:, :, t0:t0 + NT]
                hn_dst_fn = None
                b_outs = [b_y]
            else:
                x_dst_fn = lambda e, t0: fm(x_res.ap())[:, :, t0:t0 + NT]
                hn_dst_fn = lambda e, t0, i=i: fm(hn_src[i].ap())[:, :, t0:t0 + NT]
                b_hnsrc = Buf("hnsrc")
                b_outs = [b_xres, b_hnsrc]
            emit_tail(cx, HALF, x_src_fn, o_src_fn, L[i]["w_out"], L[i]["w_g"], L[i]["w_u"], L[i]["w_d"], L[i]["gains"],
                      x_dst_fn, hn_dst_fn, [b_og, b_xres], b_outs)
            if not last:
                b_hng = Buf("hng")
                cc = S.op("pool", lambda e, i=i: e.collective_compute("AllGather", ALU.bypass, replica_groups=[list(range(8))],
                                                                     ins=[hn_src[i].ap().opt()], outs=[hng[i].ap().opt()]),
                          reads=[b_hnsrc, b_xres], writes=[b_hng], dma_key=f"cc{2 * i + 1}", sem_inc=1, after_all=True)
                S.barrier(cc)
        S.op("sp", lambda e: None, reads=[b_y], after_all=True)
        S.emit(nc, es)
    return nc


_PROG = {}


def _consts(mode):
    p = np.arange(128)
    rc = np.zeros((128, 4), np.float32)
    rc[:, 0] = (10000.0 ** (-(2.0 * (p % 32)) / 64.0)).astype(np.float32)
    rc[:, 1] = np.where((p % 64) < 32, -1.0, 1.0)
    cm = np.zeros((128, 256), np.float32)
    if mode in ("A", "C"):
        for m in range(128):
            k = m + 32 if (m % 64) < 32 else m - 32
            cm[k, m] = 1.0
    else:
        jj, ss = np.meshgrid(p, p, indexing="ij")
        cm[:, 0:128] = np.where(jj >= ss, -1.0, 0.0)
    return rc, cm


def _col(v):
    return np.ascontiguousarray(np.asarray(v, np.float32).reshape(8, 128).T)


def kernel(x, positions, norm_gains, a_w_in, a_w_out, a_lambda, a_subln, b_w_in, b_w_out,
           c_w_in, c_w_out, c_sinks, ffn_w_gate, ffn_w_up, ffn_w_down, _depth=None):
    x = np.asarray(x, np.float32)
    BATCH, SEQ, _ = x.shape
    DEPTH = _depth or int(np.asarray(norm_gains).shape[0])
    positions = np.asarray(positions, np.int32)
    norm_gains = np.asarray(norm_gains, np.float32)
    key = (SEQ, DEPTH)
    if key not in _PROG:
        _PROG[key] = fused_program(SEQ, DEPTH)
    rc, cmP = _consts("A")
    _, cmT = _consts("B")
    c32 = lambda a: np.ascontiguousarray(np.asarray(a, np.float32))
    shared = {"rc": rc, "cmP": cmP, "cmT": cmT, "g0": _col(norm_gains[0, 0])}
    for i in range(DEPTH):
        gl = np.ones((4, D), np.float32)
        gl[0:3] = norm_gains[i, 1:4]
        if i + 1 < DEPTH:
            gl[3] = norm_gains[i + 1, 0]
        shared[f"gains{i}"] = np.ascontiguousarray(gl.reshape(4, 8, 128).transpose(2, 0, 1).reshape(128, 32))
        inst = i // 3
        shared[f"w_out{i}"] = c32((a_w_out, b_w_out, c_w_out)[i % 3][inst])
        shared[f"w_g{i}"] = c32(ffn_w_gate[i])
        shared[f"w_u{i}"] = c32(ffn_w_up[i])
        shared[f"w_d{i}"] = c32(ffn_w_down[i])
    in_maps = []
    for b in range(BATCH):
        xTb = np.ascontiguousarray(x[b].T)
        posb = np.ascontiguousarray(np.broadcast_to(positions[b][None, :], (128, SEQ)))
        for hh in range(2):
            m = dict(shared)
            m["xT"] = xTb
            m["pos"] = posb
            for i in range(DEPTH):
                mode, inst = "ABC"[i % 3], i // 3
                extra = np.zeros((128, 264), np.float32)
                if mode == "A":
                    w = np.asarray(a_w_in[inst], np.float32)
                    wq, wk, wv = (w[:, o + hh * 512:o + (hh + 1) * 512] for o in (0, 1024, 2048))
                    lam_init = 0.8 - 0.6 * math.exp(-0.3 * i)
                    extra[:, 0:256] = np.asarray(a_lambda[inst], np.float32).reshape(1, 256)
                    extra[:, 256] = np.asarray(a_subln[inst], np.float32)
                    extra[:, 257] = lam_init
                    extra[:, 258] = 1.0 - lam_init
                elif mode == "B":
                    w = np.asarray(b_w_in[inst], np.float32)
                    wq, wk, wv = (w[:, o + hh * 512:o + (hh + 1) * 512] for o in (0, 1024, 2048))
                else:
                    w = np.asarray(c_w_in[inst], np.float32)
                    wq = w[:, hh * 512:(hh + 1) * 512]
                    wk = w[:, 1024 + hh * 64:1024 + (hh + 1) * 64]
                    wv = w[:, 1152 + hh * 64:1152 + (hh + 1) * 64]
                    extra[:, 0:8] = np.asarray(c_sinks[inst], np.float32)[hh * 8:(hh + 1) * 8].reshape(1, 8)
                m[f"wq{i}"], m[f"wk{i}"], m[f"wv{i}"], m[f"extra{i}"] = c32(wq), c32(wk), c32(wv), extra
            in_maps.append(m)
    res = run_bass_kernel_spmd(_PROG[key], in_maps, core_ids=list(range(2 * BATCH)))
    out = np.empty((BATCH, SEQ, D), np.float32)
    half = SEQ // 2
    for b in range(BATCH):
        for r in range(2):
            out[b, r * half:(r + 1) * half, :] = res.results[2 * b + r]["yT"].T
    return out
```
